# Optimizing a Trainium2 kernel written in Bass

```python
import jax, jax.numpy as jnp
from jax import lax
import numpy as np

D_MODEL = 1024
BATCH = 2
SEQ = 8192
DEPTH = 2

D_MIX = D_MODEL
HEAD_DIM = 64
ATTN_WIDTH = D_MIX // 2
N_HEADS = ATTN_WIDTH // HEAD_DIM
N_KV_HEADS = 2
GROUP = N_HEADS // N_KV_HEADS
KV_WIDTH = N_KV_HEADS * HEAD_DIM
CONV_CHANNELS = D_MIX - ATTN_WIDTH
CONV_WIDTH = 31
WINDOW = 128
BLOCK = 128
ROPE_THETA = 10000.0
D_FF = ((8 * D_MODEL // 3 + 127) // 128) * 128
D_IN = ATTN_WIDTH + 2 * KV_WIDTH + 2 * CONV_CHANNELS
EPS = 1e-5

kernel_name = "hybrid_swa_sink_conformer_conv_macaron"


def rms_norm(x, g):
    xf = x.astype(jnp.float32)
    y = xf * lax.rsqrt(jnp.mean(xf * xf, axis=-1, keepdims=True) + EPS)
    return (y * g.astype(jnp.float32)).astype(x.dtype)


def layer_norm(x, g, b):
    xf = x.astype(jnp.float32)
    mu = jnp.mean(xf, axis=-1, keepdims=True)
    xc = xf - mu
    y = xc * lax.rsqrt(jnp.mean(xc * xc, axis=-1, keepdims=True) + EPS)
    return (y * g.astype(jnp.float32) + b.astype(jnp.float32)).astype(x.dtype)


def swiglu(h, w_gate, w_up, w_down):
    return (jax.nn.silu(h @ w_gate) * (h @ w_up)) @ w_down


def rope_tables(positions):
    inv_freq = 1.0 / (ROPE_THETA ** (jnp.arange(0, HEAD_DIM, 2, dtype=jnp.float32) / HEAD_DIM))
    ang = positions.astype(jnp.float32)[..., None] * inv_freq
    return jnp.cos(ang), jnp.sin(ang)


def apply_rope(t, cos, sin):
    tf = t.astype(jnp.float32)
    t1, t2 = jnp.split(tf, 2, axis=-1)
    c = cos[:, :, None, :]
    s = sin[:, :, None, :]
    return jnp.concatenate([t1 * c - t2 * s, t2 * c + t1 * s], axis=-1).astype(t.dtype)


def sliding_window_attention(q, k, v, sinks):
    B, S = q.shape[0], q.shape[1]
    nb = S // BLOCK
    qb = q.reshape(B, nb, BLOCK, N_KV_HEADS, GROUP, HEAD_DIM).astype(jnp.float32)

    def band(t):
        tb = t.reshape(B, nb, BLOCK, N_KV_HEADS, HEAD_DIM)
        prev = jnp.pad(tb[:, :-1], ((0, 0), (1, 0), (0, 0), (0, 0), (0, 0)))
        return jnp.concatenate([prev, tb], axis=2).astype(jnp.float32)

    kb, vb = band(k), band(v)
    scores = jnp.einsum('bnqkgd,bnjkd->bnkgqj', qb, kb) * (HEAD_DIM ** -0.5)

    q_local = jnp.arange(BLOCK)[:, None] + BLOCK
    k_local = jnp.arange(2 * BLOCK)[None, :]
    rel = q_local - k_local
    in_window = (rel >= 0) & (rel < WINDOW)
    block_valid = (jnp.arange(nb)[:, None] > 0) | (k_local >= BLOCK)
    mask = in_window[None, :, :] & block_valid[:, None, :]
    neg = jnp.finfo(jnp.float32).min
    scores = jnp.where(mask[None, :, None, None, :, :], scores, neg)

    sink = sinks.astype(jnp.float32).reshape(N_KV_HEADS, GROUP)[None, None, :, :, None, None]
    m = jnp.maximum(jnp.max(scores, axis=-1, keepdims=True), sink)
    p = jnp.exp(scores - m)
    denom = jnp.sum(p, axis=-1, keepdims=True) + jnp.exp(sink - m)
    probs = p / denom
    out = jnp.einsum('bnkgqj,bnjkd->bnqkgd', probs, vb)
    return out.reshape(B, S, N_HEADS * HEAD_DIM).astype(q.dtype)


def conformer_conv(u, conv_w, conv_b, ln_g, ln_b):
    a, gate = jnp.split(u, 2, axis=-1)
    h = a * jax.nn.sigmoid(gate)
    h = lax.conv_general_dilated(
        h, conv_w[:, None, :].astype(h.dtype),
        window_strides=(1,), padding=[(CONV_WIDTH - 1, 0)],
        dimension_numbers=('NWC', 'WIO', 'NWC'),
        feature_group_count=CONV_CHANNELS) + conv_b
    h = layer_norm(h, ln_g, ln_b)
    return jax.nn.silu(h)


def setup_inputs(seed: int = 0) -> dict:
    key = jax.random.key(seed)
    ks = jax.random.split(key, 20)
    f32 = jnp.float32

    def w(k, shape, fan_in):
        return jax.random.normal(k, shape, f32) * (fan_in ** -0.5)

    def gain(k, shape):
        return 1.0 + 0.05 * jax.random.normal(k, shape, f32)

    x = jax.random.normal(ks[0], (BATCH, SEQ, D_MODEL), f32)
    positions = jnp.broadcast_to(jnp.arange(SEQ, dtype=jnp.int32), (BATCH, SEQ))
    return {
        "x": x,
        "positions": positions,
        "ffn1_norm": gain(ks[1], (DEPTH, D_MODEL)),
        "ffn1_w_gate": w(ks[2], (DEPTH, D_MODEL, D_FF), D_MODEL),
        "ffn1_w_up": w(ks[3], (DEPTH, D_MODEL, D_FF), D_MODEL),
        "ffn1_w_down": w(ks[4], (DEPTH, D_FF, D_MODEL), D_FF),
        "mix_norm": gain(ks[5], (DEPTH, D_MODEL)),
        "w_in": w(ks[6], (DEPTH, D_MODEL, D_IN), D_MODEL),
        "conv_w": w(ks[7], (DEPTH, CONV_WIDTH, CONV_CHANNELS), CONV_WIDTH),
        "conv_b": 0.02 * jax.random.normal(ks[8], (DEPTH, CONV_CHANNELS), f32),
        "conv_ln_g": gain(ks[9], (DEPTH, CONV_CHANNELS)),
        "conv_ln_b": 0.02 * jax.random.normal(ks[10], (DEPTH, CONV_CHANNELS), f32),
        "attn_sinks": 0.5 * jax.random.normal(ks[11], (DEPTH, N_HEADS), f32),
        "w_out": w(ks[12], (DEPTH, D_MIX, D_MODEL), D_MIX),
        "ffn2_norm": gain(ks[13], (DEPTH, D_MODEL)),
        "ffn2_w_gate": w(ks[14], (DEPTH, D_MODEL, D_FF), D_MODEL),
        "ffn2_w_up": w(ks[15], (DEPTH, D_MODEL, D_FF), D_MODEL),
        "ffn2_w_down": w(ks[16], (DEPTH, D_FF, D_MODEL), D_FF),
        "final_norm": gain(ks[17], (D_MODEL,)),
    }


def reference(x, positions, ffn1_norm, ffn1_w_gate, ffn1_w_up, ffn1_w_down,
              mix_norm, w_in, conv_w, conv_b, conv_ln_g, conv_ln_b, attn_sinks, w_out,
              ffn2_norm, ffn2_w_gate, ffn2_w_up, ffn2_w_down, final_norm):
    B, S = x.shape[0], x.shape[1]
    cos, sin = rope_tables(positions)
    q_end = ATTN_WIDTH
    k_end = q_end + KV_WIDTH
    v_end = k_end + KV_WIDTH
    for l in range(DEPTH):
        x = x + 0.5 * swiglu(rms_norm(x, ffn1_norm[l]), ffn1_w_gate[l], ffn1_w_up[l], ffn1_w_down[l])
        h = rms_norm(x, mix_norm[l])
        p = h @ w_in[l]
        q = apply_rope(p[..., :q_end].reshape(B, S, N_HEADS, HEAD_DIM), cos, sin)
        k = apply_rope(p[..., q_end:k_end].reshape(B, S, N_KV_HEADS, HEAD_DIM), cos, sin)
        v = p[..., k_end:v_end].reshape(B, S, N_KV_HEADS, HEAD_DIM)
        u = p[..., v_end:]
        attn_out = sliding_window_attention(q, k, v, attn_sinks[l])
        conv_out = conformer_conv(u, conv_w[l], conv_b[l], conv_ln_g[l], conv_ln_b[l])
        x = x + jnp.concatenate([attn_out, conv_out], axis=-1) @ w_out[l]
        x = x + 0.5 * swiglu(rms_norm(x, ffn2_norm[l]), ffn2_w_gate[l], ffn2_w_up[l], ffn2_w_down[l])
    return rms_norm(x, final_norm)
```

```python
import contextlib
import numpy as np
import ml_dtypes
import concourse.bass as bass
import concourse.mybir as mybir
from concourse.bass_utils import run_bass_kernel_spmd

F32 = mybir.dt.float32
BF16 = mybir.dt.bfloat16
I32 = mybir.dt.int32
AF = mybir.ActivationFunctionType
ALU = mybir.AluOpType

ENGINES = ("tensor", "vector", "scalar", "gpsimd", "sync")
NDMA_SEMS = 6
import os as _os0
NDMA_Q = {"gpsimd": 1, "sync": int(_os0.environ.get("NDSY", "2"))}

D = 1024
DFF = 2816
NFC = DFF // 128
DIN = 1792
CW = 31
EPS = 1e-5
NEG = -30000.0


class Op:
    __slots__ = ("eng", "fn", "waits", "signal", "dma", "idx")

    def __init__(self, eng, fn, dma=None):
        self.eng = eng
        self.fn = fn
        self.waits = {}
        self.signal = False
        self.dma = dma
        self.idx = None


class Prog:
    def __init__(self, same_engine_sync=("vector", "scalar", "gpsimd")):
        self.ops = {e: [] for e in ENGINES}
        self.last_w = {}
        self.readers = {}
        self.ndma = {e: 0 for e in ENGINES}
        self.same = set(same_engine_sync)

    def _add_dep(self, op, key, val):
        if key[0] == "e" and key[1] == op.eng and key[1] not in self.same:
            return
        if op.waits.get(key, -1) < val:
            op.waits[key] = val

    def add(self, eng, fn, reads=(), writes=(), dma=False):
        if dma:
            n = self.ndma[eng]
            self.ndma[eng] += 1
            nq = NDMA_Q.get(eng, NDMA_SEMS)
            k = n % nq
            val = 16 * (n // nq + 1)
            op = Op(eng, fn, dma=(eng, k, val))
            if val > 16:
                op.waits[("d", eng, k)] = val - 16
            mykey, myval = ("d", eng, k), val
        else:
            op = Op(eng, fn)
            mykey, myval = ("e", eng), len(self.ops[eng])
        op.idx = len(self.ops[eng])
        for t in reads:
            w = self.last_w.get(t)
            if w is not None:
                self._add_dep(op, w[0], w[1])
        for t in writes:
            w = self.last_w.get(t)
            if w is not None:
                self._add_dep(op, w[0], w[1])
            for k2, v2 in self.readers.get(t, {}).items():
                self._add_dep(op, k2, v2)
        for t in reads:
            r = self.readers.setdefault(t, {})
            if r.get(mykey, -1) < myval:
                r[mykey] = myval
        for t in writes:
            self.last_w[t] = (mykey, myval)
            self.readers[t] = {}
        self.ops[eng].append(op)
        return op

    def emit(self, nc):
        for e in ENGINES:
            for op in self.ops[e]:
                for key, v in op.waits.items():
                    if key[0] == "e":
                        self.ops[key[1]][v].signal = True
        counts = {}
        for e in ENGINES:
            c = 0
            arr = []
            for op in self.ops[e]:
                if op.signal:
                    c += 1
                arr.append(c)
            counts[e] = arr
        with contextlib.ExitStack() as st:
            esem = {e: st.enter_context(nc.semaphore("s_" + e)) for e in ENGINES}
            dsem = {}
            for e in ENGINES:
                if self.ndma[e]:
                    for k in range(NDMA_SEMS):
                        dsem[(e, k)] = st.enter_context(nc.semaphore("d_%s%d" % (e, k)))
            block = st.enter_context(nc.Block())

            def make(e):
                def body(eng):
                    waited = {}
                    for op in self.ops[e]:
                        for key, v in op.waits.items():
                            if key[0] == "e":
                                sem = esem[key[1]]
                                val = counts[key[1]][v]
                            else:
                                sem = dsem[(key[1], key[2])]
                                val = v
                            if waited.get(sem.name, -1) >= val:
                                continue
                            waited[sem.name] = val
                            eng.wait_ge(sem, val)
                        ins = op.fn(eng)
                        if op.dma is not None:
                            ins.then_inc(dsem[(op.dma[0], op.dma[1])], 16)
                        elif op.signal:
                            ins.then_inc(esem[e], 1)
                    if self.ndma[e]:
                        n = self.ndma[e]
                        nq = NDMA_Q.get(e, NDMA_SEMS)
                        for k in range(nq):
                            cnt = (n - k + nq - 1) // nq
                            if cnt > 0:
                                eng.wait_ge(dsem[(e, k)], 16 * cnt)
                return body

            for e in ENGINES:
                if self.ops[e]:
                    getattr(block, e)(make(e))


WNAMES = ["ffn1_w_gate", "ffn1_w_up", "ffn1_w_down", "w_in", "w_out",
          "ffn2_w_gate", "ffn2_w_up", "ffn2_w_down"]
WSHAPES = {"ffn1_w_gate": [D, DFF], "ffn1_w_up": [D, DFF], "ffn1_w_down": [DFF, D],
           "w_in": [D, DIN], "w_out": [D, D],
           "ffn2_w_gate": [D, DFF], "ffn2_w_up": [D, DFF], "ffn2_w_down": [DFF, D]}


def build(NOWN, DEPTH=2, stop_after=None):
    NB = NOWN + 2
    T = NB * 128
    nc = bass.Bass("TRN2", target_bir_lowering=False)
    dr = lambda n, s, d, k="ExternalInput": nc.dram_tensor(n, s, d, kind=k).ap()
    x_d = dr("x", [T, D], F32)
    pos_d = dr("pos", [128, NB], I32)
    mask_d = dr("mask", [128, 3, 512], BF16)
    hv_d = dr("hv", [128, 2], F32)
    ident_d = dr("ident", [128, 128], BF16)
    invf_d = dr("invf", [128, 32], F32)
    gains_d = dr("gains", [128, 3 * DEPTH, 8], F32)
    gfin_d = dr("gfin", [D], F32)
    cvec_d = dr("cvec", [128, DEPTH, 4, 34], F32)
    sinks_d = dr("sinks", [DEPTH, 8], F32)
    W = {n: dr(n, [DEPTH] + WSHAPES[n], F32) for n in WNAMES}
    out_d = dr("out", [NOWN * 128, D], F32, "ExternalOutput")

    P = Prog()
    st = contextlib.ExitStack()
    with st:
        sb = lambda n, s, d: st.enter_context(nc.sbuf_tensor(n, s, d))
        xs = sb("xs", [128, NB, D], F32)
        hTflat = sb("hT", [128, max(8 * T, 124 * 128)], BF16)
        hT = hTflat[:, 0:8 * T].rearrange("p (c t) -> p c t", c=8)
        warena = sb("warena", [128, 26624], BF16)
        arb = sb("arb", [128, 9088], BF16)
        hnt = sb("hnt", [128, 2, 1024], BF16)
        arf = sb("arf", [128, 3712], F32)
        cosT = sb("cosT", [128, NB, 32], F32)
        sinT = sb("sinT", [128, NB, 32], F32)
        maskS = sb("maskS", [128, 3, 512], BF16)
        ident = sb("ident_s", [128, 128], BF16)
        onesf = sb("onesf", [128, 128], F32)
        cvec = sb("cvec_s", [128, DEPTH, 4, 34], F32)
        gains = sb("gains_s", [128, 3 * DEPTH, 8], F32)
        hv = sb("hv_s", [128, 2], F32)
        invf = sb("invf_s", [128, 32], F32)
        posi = sb("posi", [128, NB], I32)
        posf = sb("posf", [128, NB], F32)
        ss = sb("ss", [128, NB], F32)
        rs = sb("rs", [128, NB], F32)
        esink = sb("esink", [128, 8], F32)
        den = sb("den", [128, 8], F32)
        vx = sb("vx", [128, 3, 2, 65], BF16)
        junk = sb("junk", [128, 2], F32)
        pfb = [st.enter_context(nc.psum_tensor("pf%d" % i, [128, 512], F32)) for i in range(6)]
        pbb = [st.enter_context(nc.psum_tensor("pb%d" % i, [128, 8, 128], BF16)) for i in range(2)]
        cnt = {"pf": 0, "pb": 0}

        def pf():
            i = cnt["pf"] % 6
            cnt["pf"] += 1
            return pfb[i], ("pf", i)

        def pb():
            i = cnt["pb"] % 2
            cnt["pb"] += 1
            return pbb[i], ("pb", i)

        AR = "ARENA"

        def vb(a, b, **kw):
            v = arb[:, a:b]
            if kw:
                pat = kw.pop("pat")
                v = v.rearrange(pat, **kw)
            return v

        def vf(a, b, **kw):
            v = arf[:, a:b]
            if kw:
                pat = kw.pop("pat")
                v = v.rearrange(pat, **kw)
            return v

        hn = [hnt[:, 0, :], hnt[:, 1, :]]
        act = [vb(0, 2048, pat="p (c t) -> p c t", c=4), vb(2048, 4096, pat="p (c t) -> p c t", c=4)]
        silu_t = [vf(0, 512), vf(512, 1024)]
        PT = vb(0, 2048, pat="p (g k t) -> p g k t", g=2, k=2)
        attnT = vb(2048, 3072, pat="p (c t) -> p c t", c=4)
        convo = vb(3072, 4096, pat="p (c t) -> p c t", c=4)
        q_r = vb(4096, 4608, pat="p (h d) -> p h d", h=8)
        q_r2 = vb(4096, 4608)
        k_r = vb(4608, 4736, pat="p (h d) -> p h d", h=2)
        attn_tok = vb(4736, 5248, pat="p (h d) -> p h d", h=8)
        attn_tok2 = vb(4736, 5248)
        qT = vb(5248, 6272)
        kT = [vb(6272, 6528, pat="p (g t) -> p g t", g=2), vb(6528, 6784, pat="p (g t) -> p g t", g=2)]
        hg2 = [vb(6784, 7928, pat="p (c t) -> p c t", c=4), vb(7928, 9072, pat="p (c t) -> p c t", c=4)]
        acc = vf(0, 1024, pat="p (c t) -> p c t", c=4)
        ysq = [vf(1024, 1280), vf(1280, 1536)]
        sig = [vf(1536, 1792), vf(1792, 2048)]
        mean = vf(2048, 2304)
        rstdb = vf(2304, 2560)
        var = vf(2560, 2816)
        kf = vf(2816, 2944, pat="p (h d) -> p h d", h=2)
        ropeA = vf(2944, 3200, pat="p (h d) -> p h d", h=8)
        ropeB = vf(3200, 3456, pat="p (h d) -> p h d", h=8)
        tmpc = [vf(3456, 3712)]
        gfin_v = vf(1024, 2048)
        dg = hTflat[:, 0:124 * 128].rearrange("p (i m) -> p i m", m=128)
        S0, S1, SX = 0, 12288, 24576
        wg_v = [warena[:, s:s + 4096].rearrange("p (c f) -> p c f", c=8) for s in (S0, S1)]
        wu_v = [warena[:, s + 4096:s + 8192].rearrange("p (c f) -> p c f", c=8) for s in (S0, S1)]
        wd_v = [warena[:, s + 8192:s + 12288].rearrange("p (c d) -> p c d", c=4) for s in (S0, S1)]
        win_q = warena[:, S0:S0 + 4096].rearrange("p (c f) -> p c f", c=8)
        win_u = warena[:, S0 + 4096:S0 + 12288].rearrange("p (c f) -> p c f", c=8)
        win_kv = warena[:, SX:SX + 2048].rearrange("p (c f) -> p c f", c=8)
        wout_v = warena[:, S1:S1 + 8192].rearrange("p (c f) -> p c f", c=8)
        hTg = [warena[:, S1 + 8192 + i * 2048:S1 + 8192 + (i + 1) * 2048].rearrange("p (c t) -> p c t", c=8) for i in range(2)]

        def barrier():
            P.add("vector", lambda e: e.memset(junk[:, 0:1], 0.0), writes=[AR, "junk"])

        P.add("sync", lambda e: e.dma_start(out=ident[:], in_=ident_d), writes=["ident"], dma=True)
        P.add("sync", lambda e: e.dma_start(out=posi[:], in_=pos_d), writes=["posi"], dma=True)
        P.add("sync", lambda e: e.dma_start(out=invf[:], in_=invf_d), writes=["invf"], dma=True)
        P.add("sync", lambda e: e.dma_start(out=gains[:], in_=gains_d), writes=["gains"], dma=True)
        for b in range(NB):
            P.add("sync", lambda e, b=b: e.dma_start(out=xs[:, b, :], in_=x_d[b * 128:(b + 1) * 128, :]),
                  writes=[("x", b)], dma=True)
        P.add("sync", lambda e: e.dma_start(out=maskS[:], in_=mask_d), writes=["mask"], dma=True)
        P.add("sync", lambda e: e.dma_start(out=hv[:], in_=hv_d), writes=["hv"], dma=True)
        P.add("sync", lambda e: e.dma_start(out=cvec[:], in_=cvec_d), writes=["cvec"], dma=True)
        P.add("vector", lambda e: e.memset(onesf[:], 1.0), writes=["onesf"])
        P.add("vector", lambda e: e.memset(vx[:], 1.0), writes=[("vx", 0), ("vx", 1), ("vx", 2)])

        P.add("vector", lambda e: e.tensor_copy(out=posf[:], in_=posi[:]), reads=["posi"], writes=["posf"])
        ub = arf[:, 0:NB * 32].rearrange("p (b f) -> p b f", f=32)
        ui = arf[:, 1024:1024 + NB * 32].bitcast(I32).rearrange("p (b f) -> p b f", f=32)
        uf = arf[:, 2048:2048 + NB * 32].rearrange("p (b f) -> p b f", f=32)
        P.add("vector", lambda e: e.tensor_tensor(out=ub, in0=posf[:].unsqueeze(2).to_broadcast([128, NB, 32]),
                                                  in1=invf[:].unsqueeze(1).to_broadcast([128, NB, 32]), op=ALU.mult),
              reads=["posf", "invf", AR], writes=["ub"])
        for name, tab, off in (("sin", sinT, 0.0), ("cos", cosT, 0.25)):
            if off != 0.0:
                P.add("vector", lambda e, off=off: e.tensor_scalar(ub, ub, off, None, ALU.add), reads=["ub", AR], writes=["ub"])
            P.add("vector", lambda e: e.tensor_copy(out=ui, in_=ub), reads=["ub", AR], writes=["ui"])
            P.add("vector", lambda e: e.tensor_copy(out=uf, in_=ui), reads=["ui", AR], writes=["uf"])
            P.add("vector", lambda e: e.tensor_tensor(out=uf, in0=ub, in1=uf, op=ALU.subtract), reads=["ub", "uf", AR], writes=["uf"])
            P.add("scalar", lambda e, tab=tab: e.activation(out=tab[:], in_=uf, func=AF.Sin, scale=2.0 * np.pi),
                  reads=["uf", AR], writes=[name])
        barrier()

        sqj = vb(4096, 5120)

        def norm_blocks(gi, blocks, dest=None, part="all"):
            b0, b1 = blocks[0], blocks[-1] + 1
            assert list(blocks) == list(range(b0, b1))
            if part == "pre":
                assert len(blocks) <= 2
            for b in (blocks if part != "post" else []):
                jk, jt = (sqj, "sqj") if dest is None else (hn[b % 2], ("hn", b % 2))
                P.add("scalar", lambda e, b=b, jk=jk: e.activation(out=jk, in_=xs[:, b, :], func=AF.Square, accum_out=ss[:, b:b + 1]),
                      reads=[("x", b), AR], writes=[jt, ("ss", b)])
            sst = [("ss", b) for b in blocks]
            rst = [("rs", b) for b in blocks]
            if part != "post":
                P.add("vector", lambda e: e.tensor_scalar(rs[:, b0:b1], ss[:, b0:b1], 1.0 / D, EPS, ALU.mult, ALU.add), reads=sst, writes=rst)
                P.add("scalar", lambda e: e.activation(out=rs[:, b0:b1], in_=rs[:, b0:b1], func=AF.Sqrt), reads=rst, writes=rst)
                P.add("vector", lambda e: e.reciprocal(out=rs[:, b0:b1], in_=rs[:, b0:b1]), reads=rst, writes=rst)
            for i, b in enumerate(blocks):
                p = b % 2
                if part != "post":
                    P.add("scalar", lambda e, b=b, p=p: e.activation(out=hn[p], in_=xs[:, b, :], func=AF.Copy, scale=rs[:, b:b + 1]),
                          reads=[("x", b), ("rs", b), AR], writes=[("hn", p)])
                if part == "pre":
                    continue
                bank, bt = pb()
                for c in range(8):
                    P.add("tensor", lambda e, c=c, p=p, bank=bank: e.transpose(bank[:, c, :], hn[p][:, c * 128:(c + 1) * 128], ident[:]),
                          reads=[("hn", p), "ident", AR], writes=[bt])
                if dest is None:
                    dap, dtok = hT[:, :, b * 128:(b + 1) * 128], ("hT", b)
                else:
                    dap, dtok = dest(i)
                P.add("vector", lambda e, bank=bank, dap=dap: e.tensor_tensor(
                    out=dap, in0=bank[:],
                    in1=gains[:, gi, :].unsqueeze(2).to_broadcast([128, 8, 128]), op=ALU.mult),
                    reads=[bt, "gains", AR], writes=[dtok])

        ffn_groups_all = [[0, 1]] + [list(range(2 + 4 * i, 2 + 4 * i + 4)) for i in range(NOWN // 4)]
        fgroups = [(0, 4), (4, 4), (8, 4), (12, 4), (16, 4), (20, 2)]
        slotctr = [0]

        def ffn(l, gname, uname, dname, post_norm=None, pre_norm=None, post_hook=None, first_block=0):
            ffn_groups = [[b for b in g if b >= first_block] for g in ffn_groups_all]
            ffn_groups = [g for g in ffn_groups if g]
            for fgi, (c0, gsz) in enumerate(fgroups):
                s = slotctr[0] % 2
                slotctr[0] += 1
                f0, f1 = c0 * 128, (c0 + gsz) * 128
                P.add("gpsimd", lambda e, s=s, f0=f0, f1=f1, gsz=gsz: e.dma_start(
                    out=wg_v[s][:, :, 0:gsz * 128], in_=W[gname][l, :, f0:f1].rearrange("(c p) f -> p c f", p=128)),
                    writes=[("slot", s)], dma=True)
                P.add("gpsimd", lambda e, s=s, f0=f0, f1=f1, gsz=gsz: e.dma_start(
                    out=wu_v[s][:, :, 0:gsz * 128], in_=W[uname][l, :, f0:f1].rearrange("(c p) f -> p c f", p=128)),
                    writes=[("slotu", s)], dma=True)
                P.add("gpsimd", lambda e, s=s, f0=f0, f1=f1, gsz=gsz: e.dma_start(
                    out=wd_v[s][:, 0:gsz, :], in_=W[dname][l, f0:f1, :].rearrange("(c p) d -> p c d", p=128)),
                    reads=([AR] if s == 1 else []), writes=[("slotd", s)], dma=True)

                def gu(ti, s=s, gsz=gsz):
                    blocks = ffn_groups[ti]
                    t0, Tg = blocks[0] * 128, len(blocks) * 128
                    p = ti % 2
                    hreads = [("hT", b) for b in blocks]
                    for j in range(gsz):
                        ba, ta = pf()
                        bu, tu = pf()
                        for c in range(8):
                            P.add("tensor", lambda e, c=c, j=j, ba=ba: e.matmul(
                                ba[:, 0:Tg], lhsT=wg_v[s][:, c, j * 128:(j + 1) * 128], rhs=hT[:, c, t0:t0 + Tg],
                                start=(c == 0), stop=(c == 7)), reads=hreads + [("slot", s), AR], writes=[ta])
                        for c in range(8):
                            P.add("tensor", lambda e, c=c, j=j, bu=bu: e.matmul(
                                bu[:, 0:Tg], lhsT=wu_v[s][:, c, j * 128:(j + 1) * 128], rhs=hT[:, c, t0:t0 + Tg],
                                start=(c == 0), stop=(c == 7)), reads=hreads + [("slotu", s), AR], writes=[tu])
                        sp = j % 2
                        P.add("scalar", lambda e, ba=ba, sp=sp: e.activation(out=silu_t[sp][:, 0:Tg], in_=ba[:, 0:Tg], func=AF.Silu),
                              reads=[ta, AR], writes=[("silu", sp)])
                        P.add("vector", lambda e, bu=bu, sp=sp, j=j, p=p: e.tensor_tensor(
                            out=act[p][:, j, 0:Tg], in0=bu[:, 0:Tg], in1=silu_t[sp][:, 0:Tg], op=ALU.mult),
                            reads=[tu, ("silu", sp), AR], writes=[("act", p, j)])

                def down(ti, s=s, gsz=gsz):
                    blocks = ffn_groups[ti]
                    p = ti % 2
                    for bl, b in enumerate(blocks):
                        for half in range(2):
                            bk, tk = pf()
                            for j in range(gsz):
                                P.add("tensor", lambda e, j=j, bl=bl, half=half, bk=bk: e.matmul(
                                    bk[:, :], lhsT=act[p][:, j, bl * 128:(bl + 1) * 128],
                                    rhs=wd_v[s][:, j, half * 512:(half + 1) * 512],
                                    start=(j == 0), stop=(j == gsz - 1)),
                                    reads=[("act", p, j), ("slotd", s), AR], writes=[tk])
                            P.add("vector", lambda e, b=b, half=half, bk=bk: e.scalar_tensor_tensor(
                                out=xs[:, b, half * 512:(half + 1) * 512], in0=bk[:, :], scalar=0.5,
                                in1=xs[:, b, half * 512:(half + 1) * 512], op0=ALU.mult, op1=ALU.add),
                                reads=[tk, ("x", b)], writes=[("x", b)])

                ng = len(ffn_groups)
                pn = pre_norm is not None and fgi == 0
                if pn:
                    norm_blocks(pre_norm, ffn_groups[0])
                    if ng > 1:
                        norm_blocks(pre_norm, ffn_groups[1])
                gu(0)
                for ti in range(ng):
                    if pn and ti + 2 < ng:
                        norm_blocks(pre_norm, ffn_groups[ti + 2])
                    if ti + 1 < ng:
                        gu(ti + 1)
                    down(ti)
                    if fgi == len(fgroups) - 1:
                        if post_norm is not None:
                            norm_blocks(post_norm, ffn_groups[ti])
                        if post_hook is not None:
                            post_hook(ffn_groups[ti])

        mix_groups = [[2 * i, 2 * i + 1] for i in range(NB // 2)]

        def mixer(l):
            P.add("gpsimd", lambda e: e.dma_start(out=win_q, in_=W["w_in"][l, :, 0:512].rearrange("(c p) f -> p c f", p=128)),
                  writes=[("slot", 0)], dma=True)
            P.add("gpsimd", lambda e: e.dma_start(out=win_kv, in_=W["w_in"][l, :, 512:768].rearrange("(c p) f -> p c f", p=128)),
                  writes=["slotx"], dma=True)
            P.add("gpsimd", lambda e: e.dma_start(out=win_u, in_=W["w_in"][l, :, 768:1792].rearrange("(c p) f -> p c f", p=128)),
                  writes=[("slotu", 0), ("slotd", 0)], dma=True)
            P.add("gpsimd", lambda e: e.dma_start(out=wout_v, in_=W["w_out"][l].rearrange("(c p) f -> p c f", p=128)),
                  writes=[("slot", 1), ("slotu", 1), ("slotd", 1)], dma=True)
            WQ, WU, WKV, WO = [("slot", 0)], [("slotu", 0), ("slotd", 0)], ["slotx"], [("slot", 1), ("slotu", 1), ("slotd", 1)]
            P.add("sync", lambda e: e.dma_start(out=esink[:], in_=sinks_d[l].partition_broadcast(128)), writes=["esink"], dma=True)
            P.add("scalar", lambda e: e.activation(out=esink[:], in_=esink[:], func=AF.Exp), reads=["esink"], writes=["esink"])
            cv = lambda c, k: cvec[:, l, c, k:k + 1]
            dg_built = set()
            def group_gen(gi, blocks):
                gp = gi % 2
                hg, hgo = hg2[gp], hg2[1 - gp]
                light = (l == DEPTH - 1 and gi == 0)
                ndest = lambda i, gp=gp: (hTg[gp][:, :, i * 128:(i + 1) * 128], ("hTg", gp, i))
                norm_blocks(3 * l + 1, blocks, dest=ndest, part="pre")
                yield
                norm_blocks(3 * l + 1, blocks, dest=ndest, part="post")
                yield
                hreads = [("hTg", gp, 0), ("hTg", gp, 1), AR]
                if gi == 0:
                    P.add("vector", lambda e: e.memset(hg[:, :, 0:30], 0.0), reads=[AR], writes=[("hg", gp)])
                else:
                    P.add("vector", lambda e: e.tensor_copy(out=hg[:, :, 0:30], in_=hgo[:, :, 256:286]), reads=[AR, ("hg", 1 - gp)], writes=[("hg", gp)])
                hvc = hv[:, 0:1] if gi == 0 else hv[:, 1:2]
                for c in range(4):
                    bk, tk = pf()
                    for kc in range(8):
                        P.add("tensor", lambda e, c=c, kc=kc, bk=bk, gp=gp: e.matmul(
                            bk[:, 0:256], lhsT=win_u[:, kc, c * 128:(c + 1) * 128], rhs=hTg[gp][:, kc, :],
                            start=(kc == 0), stop=(kc == 7)), reads=hreads + WU, writes=[tk])
                    for kc in range(8):
                        P.add("tensor", lambda e, c=c, kc=kc, bk=bk, gp=gp: e.matmul(
                            bk[:, 256:512], lhsT=win_u[:, kc, 512 + c * 128:512 + (c + 1) * 128], rhs=hTg[gp][:, kc, :],
                            start=(kc == 0), stop=(kc == 7)), reads=hreads + WU, writes=[tk])
                    sp = c % 2
                    P.add("scalar", lambda e, bk=bk, sp=sp: e.activation(out=sig[sp], in_=bk[:, 256:512], func=AF.Sigmoid),
                          reads=[tk, AR], writes=[("sig", sp)])
                    P.add("vector", lambda e, bk=bk, sp=sp, c=c, hvc=hvc: e.scalar_tensor_tensor(
                        out=hg[:, c, 30:286], in0=bk[:, 0:256], scalar=hvc, in1=sig[sp], op0=ALU.mult, op1=ALU.mult),
                        reads=[tk, ("sig", sp), "hv", AR, ("hg", gp)], writes=[("hg", gp), ("hgc", gp, c)])
                yield
                conv_pieces = []
                cbank = {}

                def conv_piece(c):
                    if c not in dg_built:
                        dg_built.add(c)
                        for k in range(CW):
                            P.add("vector", lambda e, k=k: e.tensor_scalar(dg[:, c * CW + k, :], ident[:], cv(c, k), None, ALU.mult),
                                  reads=["ident", "cvec", AR], writes=[("dg", c)])
                    cp, ch = divmod(c, 2)
                    if ch == 0:
                        cbank[cp] = pf()
                    bkc, tkc = cbank[cp]
                    for k in range(CW):
                        P.add("tensor", lambda e, k=k: e.matmul(
                            bkc[:, ch * 256:(ch + 1) * 256], lhsT=dg[:, c * CW + k, :], rhs=hg[:, c, k:k + 256],
                            start=(k == 0), stop=(k == CW - 1)),
                            reads=[("dg", c), ("hgc", gp, c), ("hg", gp), AR], writes=[tkc])
                    if ch == 1:
                        for ch2 in range(2):
                            c2 = 2 * cp + ch2
                            P.add("scalar", lambda e, c2=c2, ch2=ch2: e.activation(
                                out=acc[:, c2, :], in_=bkc[:, ch2 * 256:(ch2 + 1) * 256], func=AF.Identity, bias=cv(c2, 31)),
                                reads=[tkc, "cvec", AR], writes=[("acc", c2)])
                if not light:
                    for c in range(4):
                        conv_pieces.append(lambda c=c: conv_piece(c))
                def block_gen(bl, b):
                    if light and b == 0:
                        return
                    par = b % 2
                    vs, vsp = b % 3, (b - 1) % 3
                    qp, tq = pf()
                    for kc in range(8):
                        P.add("tensor", lambda e, kc=kc: e.matmul(
                            qp[:, :], lhsT=hTg[gp][:, kc, bl * 128:(bl + 1) * 128], rhs=win_q[:, kc, :],
                            start=(kc == 0), stop=(kc == 7)), reads=[("hTg", gp, bl), AR] + WQ, writes=[tq])
                    kvp, tkv = pf()
                    for kc in range(8):
                        P.add("tensor", lambda e, kc=kc: e.matmul(
                            kvp[:, 0:256], lhsT=hTg[gp][:, kc, bl * 128:(bl + 1) * 128], rhs=win_kv[:, kc, :],
                            start=(kc == 0), stop=(kc == 7)), reads=[("hTg", gp, bl), AR] + WKV, writes=[tkv])
                    P.add("scalar", lambda e: e.copy(
                        out=vx[:, vs, :, 0:64], in_=kvp[:, 128:256].rearrange("p (g d) -> p g d", g=2)),
                        reads=[tkv], writes=[("vx", vs)])
                    yield
                    cb = cosT[:, b, :].unsqueeze(1).to_broadcast([128, 8, 32])
                    sbq = sinT[:, b, :].unsqueeze(1).to_broadcast([128, 8, 32])
                    cb2 = cosT[:, b, :].unsqueeze(1).to_broadcast([128, 2, 32])
                    sb2 = sinT[:, b, :].unsqueeze(1).to_broadcast([128, 2, 32])
                    q3 = qp[:, :].rearrange("p (h d) -> p h d", h=8)
                    k3 = kvp[:, 0:128].rearrange("p (h d) -> p h d", h=2)
                    TT = lambda out, a, bb, op, reads, writes: P.add(
                        "vector", lambda e: e.tensor_tensor(out=out, in0=a, in1=bb, op=op), reads=reads + [AR], writes=writes)
                    ropeC = ysq[0].rearrange("p (h d) -> p h d", h=8)
                    ropeD = ysq[1].rearrange("p (h d) -> p h d", h=8)
                    YC, YD = ("ysq", 0), ("ysq", 1)
                    TT(ropeA, q3[:, :, 0:32], cb, ALU.mult, [tq, "cos"], ["ropeA"])
                    TT(ropeB, q3[:, :, 32:64], sbq, ALU.mult, [tq, "sin"], ["ropeB"])
                    TT(ropeC, q3[:, :, 32:64], cb, ALU.mult, [tq, "cos"], [YC])
                    TT(ropeD, q3[:, :, 0:32], sbq, ALU.mult, [tq, "sin"], [YD])
                    TT(q_r[:, :, 0:32], ropeA, ropeB, ALU.subtract, ["ropeA", "ropeB"], [("q_r", 0)])
                    TT(q_r[:, :, 32:64], ropeC, ropeD, ALU.add, [YC, YD], [("q_r", 1)])
                    rA2, rB2, rC2, rD2 = ropeA[:, 0:2, :], ropeB[:, 0:2, :], ropeC[:, 0:2, :], ropeD[:, 0:2, :]
                    TT(rA2, k3[:, :, 0:32], cb2, ALU.mult, [tkv, "cos"], ["ropeA"])
                    TT(rB2, k3[:, :, 32:64], sb2, ALU.mult, [tkv, "sin"], ["ropeB"])
                    TT(rC2, k3[:, :, 32:64], cb2, ALU.mult, [tkv, "cos"], [YC])
                    TT(rD2, k3[:, :, 0:32], sb2, ALU.mult, [tkv, "sin"], [YD])
                    TT(k_r[:, :, 0:32], rA2, rB2, ALU.subtract, ["ropeA", "ropeB"], [("k_r", 0)])
                    TT(k_r[:, :, 32:64], rC2, rD2, ALU.add, [YC, YD], [("k_r", 1)])
                    yield
                    pq, tpq = pb()
                    for h in range(8):
                        P.add("tensor", lambda e, h=h: e.transpose(pq[0:64, h, :], q_r[:, h, :], ident[:]),
                              reads=[("q_r", 0), ("q_r", 1), "ident", AR], writes=[tpq])
                    P.add("scalar", lambda e: e.copy(out=qT[0:64, :], in_=pq[0:64, :, :].rearrange("p h t -> p (h t)")),
                          reads=[tpq, AR], writes=["qT"])
                    pk, tpk = pb()
                    for g in range(2):
                        P.add("tensor", lambda e, g=g: e.transpose(pk[0:64, g, :], k_r[:, g, :], ident[:]),
                              reads=[("k_r", 0), ("k_r", 1), "ident", AR], writes=[tpk])
                    P.add("scalar", lambda e: e.copy(out=kT[par][0:64, :, :], in_=pk[0:64, 0:2, :]),
                          reads=[tpk, AR], writes=[("kT", par)])
                    if light:
                        return
                    yield
                    kbs = ([] if b == 0 else [(0, 1 - par, 2 if b == 2 else 1, vsp)]) + [(1, par, 0, vs)]
                    for g in range(2):
                        for (ki, kpar, mi, vsl) in kbs:
                            stb, tst = pf()
                            P.add("tensor", lambda e, g=g, kpar=kpar, stb=stb: e.matmul(
                                stb[:, :], lhsT=kT[kpar][0:64, g, :], rhs=qT[0:64, g * 512:(g + 1) * 512],
                                start=True, stop=False), reads=[("kT", kpar), "qT", AR], writes=[tst])
                            P.add("tensor", lambda e, mi=mi, stb=stb: e.matmul(
                                stb[:, :], lhsT=ident[:], rhs=maskS[:, mi, :], start=False, stop=True),
                                reads=["ident", "mask"], writes=[tst])
                            P.add("scalar", lambda e, g=g, ki=ki, stb=stb: e.activation(
                                out=PT[:, g, ki, :], in_=stb[:, :], func=AF.Exp, scale=0.125),
                                reads=[tst, AR], writes=[("PT", g, ki)])
                    yield
                    ovs = []
                    for g in range(2):
                        ov, tov = pf()
                        ovs.append((ov, tov))
                        for hl in range(4):
                            for n, (ki, kpar, mi, vsl) in enumerate(kbs):
                                P.add("tensor", lambda e, g=g, hl=hl, ki=ki, vsl=vsl, ov=ov, n=n, last=(n == len(kbs) - 1): e.matmul(
                                    ov[:, hl * 65:(hl + 1) * 65], lhsT=PT[:, g, ki, hl * 128:(hl + 1) * 128],
                                    rhs=vx[:, vsl, g, :], start=(n == 0), stop=last),
                                    reads=[("PT", g, ki), ("vx", vsl), AR], writes=[tov])
                        ov3 = ov[:, 0:260].rearrange("p (h d) -> p h d", h=4)
                        P.add("vector", lambda e, g=g, ov3=ov3: e.tensor_tensor(
                            out=den[:, 4 * g:4 * g + 4], in0=ov3[:, :, 64], in1=esink[:, 4 * g:4 * g + 4], op=ALU.add),
                            reads=[tov, "esink"], writes=[("den", g)])
                    P.add("vector", lambda e: e.reciprocal(out=den[:], in_=den[:]), reads=[("den", 0), ("den", 1)],
                          writes=[("den", 0), ("den", 1)])
                    for g in range(2):
                        ov, tov = ovs[g]
                        ov3 = ov[:, 0:260].rearrange("p (h d) -> p h d", h=4)
                        P.add("vector", lambda e, g=g, ov3=ov3: e.tensor_tensor(
                            out=attn_tok[:, 4 * g:4 * g + 4, :], in0=ov3[:, :, 0:64],
                            in1=den[:, 4 * g:4 * g + 4].unsqueeze(2).to_broadcast([128, 4, 64]), op=ALU.mult),
                            reads=[tov, ("den", g), AR], writes=[("atok", g)])
                    yield
                    pa, tpa = pb()
                    for c in range(4):
                        P.add("tensor", lambda e, c=c: e.transpose(pa[:, c, :], attn_tok2[:, c * 128:(c + 1) * 128], ident[:]),
                              reads=[("atok", 0), ("atok", 1), "ident", AR], writes=[tpa])
                    P.add("scalar", lambda e: e.copy(out=attnT[:, :, bl * 128:(bl + 1) * 128], in_=pa[:, 0:4, :]),
                          reads=[tpa, AR], writes=[("attnT", bl)])

                bg = [block_gen(bl, b) for bl, b in enumerate(blocks)]

                def step(i):
                    try:
                        next(bg[i])
                    except StopIteration:
                        pass

                def cpiece():
                    if conv_pieces:
                        conv_pieces.pop(0)()
                step(0)
                yield
                first_c = [True]
                for item in (1, 0, 0, "c", 1, 0, 1, "c", 0, 1, "c", 0, 1, "c", 1, 0, 1):
                    if item == "c":
                        cpiece()
                        if first_c[0]:
                            first_c[0] = False
                            mid_hook(gi)
                    else:
                        step(item)
                if light:
                    yield
                    return
                while conv_pieces:
                    conv_pieces.pop(0)()
                b1, t1 = pf()
                b2, t2 = pf()
                for c in range(4):
                    sp = c % 2
                    P.add("scalar", lambda e, c=c, sp=sp: e.activation(out=ysq[sp], in_=acc[:, c, :], func=AF.Square),
                          reads=[("acc", c), AR], writes=[("ysq", sp)])
                    P.add("tensor", lambda e, c=c, b1=b1: e.matmul(b1[:, 0:256], lhsT=onesf[:], rhs=acc[:, c, :], start=(c == 0), stop=(c == 3)),
                          reads=[("acc", c), "onesf", AR], writes=[t1])
                    P.add("tensor", lambda e, c=c, sp=sp, b2=b2: e.matmul(b2[:, 0:256], lhsT=onesf[:], rhs=ysq[sp], start=(c == 0), stop=(c == 3)),
                          reads=[("ysq", sp), "onesf", AR], writes=[t2])
                yield
                P.add("vector", lambda e, b1=b1: e.tensor_scalar(mean, b1[:, 0:256], 1.0 / 512, None, ALU.mult), reads=[t1, AR], writes=["mean"])
                P.add("vector", lambda e: e.tensor_tensor(out=var, in0=mean, in1=mean, op=ALU.mult), reads=["mean", AR], writes=["var"])
                P.add("vector", lambda e, b2=b2: e.scalar_tensor_tensor(out=var, in0=b2[:, 0:256], scalar=1.0 / 512, in1=var,
                                                                 op0=ALU.mult, op1=ALU.subtract), reads=[t2, "var", AR], writes=["var"])
                P.add("vector", lambda e: e.tensor_scalar(var, var, EPS, None, ALU.add), reads=["var", AR], writes=["var"])
                P.add("scalar", lambda e: e.activation(out=var, in_=var, func=AF.Sqrt), reads=["var", AR], writes=["var"])
                P.add("vector", lambda e: e.reciprocal(out=rstdb, in_=var), reads=["var", AR], writes=["rstdb"])
                for c in range(4):
                    tp, tpt = (tmpc[0], "tmpc0") if c % 2 == 0 else (vf(2944, 3200), "ropeA")
                    P.add("vector", lambda e, c=c, tp=tp: e.tensor_tensor(out=tp, in0=acc[:, c, :], in1=mean, op=ALU.subtract),
                          reads=[("acc", c), "mean", AR], writes=[tpt])
                    P.add("vector", lambda e, tp=tp: e.tensor_tensor(out=tp, in0=tp, in1=rstdb, op=ALU.mult),
                          reads=[tpt, "rstdb", AR], writes=[tpt])
                    P.add("scalar", lambda e, c=c, tp=tp: e.activation(out=convo[:, c, :], in_=tp, func=AF.Silu,
                                                                        scale=cv(c, 32), bias=cv(c, 33)),
                          reads=[tpt, "cvec", AR], writes=[("convo", c)])
                xos = {}
                for bl, b in enumerate(blocks):
                    for half in range(2):
                        xo, txo = pf()
                        xos[(bl, half)] = (xo, txo)
                        for c in range(4):
                            P.add("tensor", lambda e, c=c, bl=bl, half=half, xo=xo: e.matmul(
                                xo[:, :], lhsT=attnT[:, c, bl * 128:(bl + 1) * 128], rhs=wout_v[:, c, half * 512:(half + 1) * 512],
                                start=(c == 0), stop=False), reads=[("attnT", bl), AR] + WO, writes=[txo])
                yield
                for bl, b in enumerate(blocks):
                    for half in range(2):
                        xo, txo = xos[(bl, half)]
                        for c in range(4):
                            P.add("tensor", lambda e, c=c, bl=bl, half=half, xo=xo: e.matmul(
                                xo[:, :], lhsT=convo[:, c, bl * 128:(bl + 1) * 128], rhs=wout_v[:, 4 + c, half * 512:(half + 1) * 512],
                                start=False, stop=(c == 3)), reads=[("convo", c), AR] + WO, writes=[txo])
                        P.add("vector", lambda e, b=b, half=half, xo=xo: e.tensor_tensor(
                            out=xs[:, b, half * 512:(half + 1) * 512], in0=xo[:, :], in1=xs[:, b, half * 512:(half + 1) * 512], op=ALU.add),
                            reads=[txo, ("x", b)], writes=[("x", b)])

            gens = [group_gen(gi, blocks) for gi, blocks in enumerate(mix_groups)]

            def mid_hook(gi):
                adv(gi + 1)

            def adv(i):
                if i < len(gens):
                    try:
                        next(gens[i])
                    except StopIteration:
                        pass
            adv(0)
            adv(0)
            adv(0)
            adv(0)
            for gi in range(len(gens)):
                adv(gi + 1)
                adv(gi)
                adv(gi + 1)
                adv(gi)
                adv(gi + 1)
                for _ in gens[gi]:
                    pass

        gfin_loaded = [False]

        def final_out(blocks):
            if not gfin_loaded[0]:
                gfin_loaded[0] = True
                P.add("sync", lambda e: e.dma_start(out=gfin_v, in_=gfin_d.partition_broadcast(128)), reads=[AR], writes=["gfin"], dma=True)
            for b in blocks:
                if b < 2:
                    continue
                i = b - 2
                P.add("scalar", lambda e, b=b: e.activation(out=hn[0], in_=xs[:, b, :], func=AF.Square, accum_out=ss[:, b:b + 1]),
                      reads=[("x", b), AR], writes=[("hn", 0), ("ss", b)])
                P.add("vector", lambda e, b=b: e.tensor_scalar(rs[:, b:b + 1], ss[:, b:b + 1], 1.0 / D, EPS, ALU.mult, ALU.add),
                      reads=[("ss", b)], writes=[("rs", b)])
                P.add("scalar", lambda e, b=b: e.activation(out=rs[:, b:b + 1], in_=rs[:, b:b + 1], func=AF.Sqrt),
                      reads=[("rs", b)], writes=[("rs", b)])
                P.add("vector", lambda e, b=b: e.reciprocal(out=rs[:, b:b + 1], in_=rs[:, b:b + 1]),
                      reads=[("rs", b)], writes=[("rs", b)])
                P.add("vector", lambda e, b=b: e.scalar_tensor_tensor(out=xs[:, b, :], in0=xs[:, b, :], scalar=rs[:, b:b + 1],
                                                                      in1=gfin_v, op0=ALU.mult, op1=ALU.mult),
                      reads=[("x", b), ("rs", b), "gfin", AR], writes=[("x", b)])
                P.add("sync", lambda e, i=i, b=b: e.dma_start(out=out_d[i * 128:(i + 1) * 128, :], in_=xs[:, b, :]),
                      reads=[("x", b)], dma=True)

        allb = list(range(NB))
        stage = [0]

        def done():
            stage[0] += 1
            return stop_after is not None and stage[0] >= stop_after

        finished = False
        import os as _os
        KS = int(_os.environ.get("KSTOP", "0"))
        for l in range(DEPTH):
            if KS == 1:
                finished = True
                break
            if KS == 2:
                finished = True
                break
            ffn(l, "ffn1_w_gate", "ffn1_w_up", "ffn1_w_down", pre_norm=(0 if l == 0 else None), first_block=(0 if l == 0 else 1))
            if done():
                finished = True
                break
            barrier()
            mixer(l)
            barrier()
            if done():
                finished = True
                break
            last = (l + 1 == DEPTH) and stop_after is None
            ffn(l, "ffn2_w_gate", "ffn2_w_up", "ffn2_w_down", post_norm=(3 * (l + 1) if l + 1 < DEPTH else None),
                pre_norm=3 * l + 2, post_hook=(final_out if last else None), first_block=(2 if l + 1 == DEPTH else 1))
            if done():
                finished = True
                break
        barrier()
        if finished:
            for i in range(NOWN):
                b = i + 2
                P.add("sync", lambda e, i=i, b=b: e.dma_start(out=out_d[i * 128:(i + 1) * 128, :], in_=xs[:, b, :]),
                      reads=[("x", b)], dma=True)
        else:
            pass
        P.emit(nc)
    return nc


def make_inputs(inputs, n_cores, cores_per_seq, NOWN, DEPTH):
    x = np.asarray(inputs["x"], np.float32)
    pos = np.asarray(inputs["positions"], np.int32)
    B, S, _ = x.shape
    own = NOWN * 128
    NB = NOWN + 2
    bf = ml_dtypes.bfloat16
    i = np.arange(128)[None, :]
    j = np.arange(128)[:, None]
    m_cur = np.where(i >= j, 0.0, NEG).astype(np.float32)
    m_prev = np.where(j > i, 0.0, NEG).astype(np.float32)
    m_none = np.full((128, 128), NEG, np.float32)
    inv_freq = (1.0 / (10000.0 ** (np.arange(0, 64, 2, dtype=np.float32) / 64))).astype(np.float32)
    invf = np.broadcast_to((inv_freq.astype(np.float64) / (2 * np.pi)).astype(np.float32)[None, :], (128, 32)).copy()
    gains = np.zeros((128, 3 * DEPTH, 8), np.float32)
    for l in range(DEPTH):
        for k, n in enumerate(["ffn1_norm", "mix_norm", "ffn2_norm"]):
            gains[:, 3 * l + k, :] = np.asarray(inputs[n], np.float32)[l].reshape(8, 128).T
    cvec = np.zeros((128, DEPTH, 4, 34), np.float32)
    for l in range(DEPTH):
        cw = np.asarray(inputs["conv_w"], np.float32)[l]
        cvec[:, l, :, 0:31] = cw.T.reshape(4, 128, 31).transpose(1, 0, 2)
        for k, n in ((31, "conv_b"), (32, "conv_ln_g"), (33, "conv_ln_b")):
            cvec[:, l, :, k] = np.asarray(inputs[n], np.float32)[l].reshape(4, 128).T
    shared = {
        "ident": np.eye(128).astype(bf), "invf": invf, "gains": gains, "cvec": cvec,
        "gfin": np.asarray(inputs["final_norm"], np.float32),
        "sinks": np.asarray(inputs["attn_sinks"], np.float32),
    }
    for n in WNAMES:
        shared[n] = np.ascontiguousarray(np.asarray(inputs[n], np.float32))
    in_maps = []
    for core in range(n_cores):
        bi, ci = divmod(core, cores_per_seq)
        t0 = ci * own
        xc = np.zeros((NB * 128, D), np.float32)
        pc = np.zeros((NB * 128,), np.int32)
        lo = t0 - 256
        if lo >= 0:
            xc[:] = x[bi, lo:t0 + own]
            pc[:] = pos[bi, lo:t0 + own]
            start = False
        else:
            xc[256:] = x[bi, t0:t0 + own]
            pc[256:] = pos[bi, t0:t0 + own]
            start = True
        mask = np.stack([m_cur, m_prev, m_none if start else m_prev], axis=1)
        mask = np.repeat(mask[:, :, None, :], 4, axis=2).reshape(128, 3, 512).astype(bf)
        hvv = np.ones((128, 2), np.float32)
        hvv[:, 0] = 0.0 if start else 1.0
        m = dict(shared)
        m.update({"x": xc, "pos": np.ascontiguousarray(pc.reshape(NB, 128).T), "mask": mask, "hv": hvv})
        in_maps.append(m)
    return in_maps


def run(inputs, n_cores, cores_per_seq, NOWN, DEPTH=2, stop_after=None, trace=False):
    nc = build(NOWN, DEPTH, stop_after)
    in_maps = make_inputs(inputs, n_cores, cores_per_seq, NOWN, DEPTH)
    res = run_bass_kernel_spmd(nc, in_maps, core_ids=list(range(n_cores)), trace=trace)
    x = np.asarray(inputs["x"])
    B, S, _ = x.shape
    own = NOWN * 128
    out = np.zeros((B, S, D), np.float32)
    for core in range(n_cores):
        bi, ci = divmod(core, cores_per_seq)
        out[bi, ci * own:(ci + 1) * own] = res.results[core]["out"]
    return out, res


def kernel(**inputs):
    out, _ = run(inputs, 8, 4, 16, 2)
    return out
```

```python
import contextlib
import numpy as np
import ml_dtypes
import concourse.bass as bass
import concourse.mybir as mybir
from concourse.bass_utils import run_bass_kernel_spmd

F32 = mybir.dt.float32
BF16 = mybir.dt.bfloat16
I32 = mybir.dt.int32
AF = mybir.ActivationFunctionType
ALU = mybir.AluOpType

ENGINES = ("tensor", "vector", "scalar", "gpsimd", "sync")
NDMA_SEMS = 6
import os as _os0
NDMA_Q = {"gpsimd": 1, "sync": int(_os0.environ.get("NDSY", "2"))}

D = 1024
DFF = 2816
NFC = DFF // 128
DIN = 1792
CW = 31
EPS = 1e-5
NEG = -30000.0


class Op:
    __slots__ = ("eng", "fn", "waits", "signal", "dma", "idx")

    def __init__(self, eng, fn, dma=None):
        self.eng = eng
        self.fn = fn
        self.waits = {}
        self.signal = False
        self.dma = dma
        self.idx = None


class Prog:
    def __init__(self, same_engine_sync=("vector", "scalar", "gpsimd")):
        self.ops = {e: [] for e in ENGINES}
        self.last_w = {}
        self.readers = {}
        self.ndma = {e: 0 for e in ENGINES}
        self.same = set(same_engine_sync)

    def _add_dep(self, op, key, val):
        if key[0] == "e" and key[1] == op.eng and key[1] not in self.same:
            return
        if op.waits.get(key, -1) < val:
            op.waits[key] = val

    def add(self, eng, fn, reads=(), writes=(), dma=False):
        if dma:
            n = self.ndma[eng]
            self.ndma[eng] += 1
            nq = NDMA_Q.get(eng, NDMA_SEMS)
            k = n % nq
            val = 16 * (n // nq + 1)
            op = Op(eng, fn, dma=(eng, k, val))
            if val > 16:
                op.waits[("d", eng, k)] = val - 16
            mykey, myval = ("d", eng, k), val
        else:
            op = Op(eng, fn)
            mykey, myval = ("e", eng), len(self.ops[eng])
        op.idx = len(self.ops[eng])
        for t in reads:
            w = self.last_w.get(t)
            if w is not None:
                self._add_dep(op, w[0], w[1])
        for t in writes:
            w = self.last_w.get(t)
            if w is not None:
                self._add_dep(op, w[0], w[1])
            for k2, v2 in self.readers.get(t, {}).items():
                self._add_dep(op, k2, v2)
        for t in reads:
            r = self.readers.setdefault(t, {})
            if r.get(mykey, -1) < myval:
                r[mykey] = myval
        for t in writes:
            self.last_w[t] = (mykey, myval)
            self.readers[t] = {}
        self.ops[eng].append(op)
        return op

    def emit(self, nc):
        for e in ENGINES:
            for op in self.ops[e]:
                for key, v in op.waits.items():
                    if key[0] == "e":
                        self.ops[key[1]][v].signal = True
        counts = {}
        for e in ENGINES:
            c = 0
            arr = []
            for op in self.ops[e]:
                if op.signal:
                    c += 1
                arr.append(c)
            counts[e] = arr
        with contextlib.ExitStack() as st:
            esem = {e: st.enter_context(nc.semaphore("s_" + e)) for e in ENGINES}
            dsem = {}
            for e in ENGINES:
                if self.ndma[e]:
                    for k in range(NDMA_SEMS):
                        dsem[(e, k)] = st.enter_context(nc.semaphore("d_%s%d" % (e, k)))
            block = st.enter_context(nc.Block())

            def make(e):
                def body(eng):
                    waited = {}
                    for op in self.ops[e]:
                        for key, v in op.waits.items():
                            if key[0] == "e":
                                sem = esem[key[1]]
                                val = counts[key[1]][v]
                            else:
                                sem = dsem[(key[1], key[2])]
                                val = v
                            if waited.get(sem.name, -1) >= val:
                                continue
                            waited[sem.name] = val
                            eng.wait_ge(sem, val)
                        ins = op.fn(eng)
                        if op.dma is not None:
                            ins.then_inc(dsem[(op.dma[0], op.dma[1])], 16)
                        elif op.signal:
                            ins.then_inc(esem[e], 1)
                    if self.ndma[e]:
                        n = self.ndma[e]
                        nq = NDMA_Q.get(e, NDMA_SEMS)
                        for k in range(nq):
                            cnt = (n - k + nq - 1) // nq
                            if cnt > 0:
                                eng.wait_ge(dsem[(e, k)], 16 * cnt)
                return body

            for e in ENGINES:
                if self.ops[e]:
                    getattr(block, e)(make(e))


WNAMES = ["ffn1_w_gate", "ffn1_w_up", "ffn1_w_down", "w_in", "w_out",
          "ffn2_w_gate", "ffn2_w_up", "ffn2_w_down"]
WSHAPES = {"ffn1_w_gate": [D, DFF], "ffn1_w_up": [D, DFF], "ffn1_w_down": [DFF, D],
           "w_in": [D, DIN], "w_out": [D, D],
           "ffn2_w_gate": [D, DFF], "ffn2_w_up": [D, DFF], "ffn2_w_down": [DFF, D]}


def build(NOWN, DEPTH=2, stop_after=None):
    NB = NOWN + 2
    T = NB * 128
    nc = bass.Bass("TRN2", target_bir_lowering=False)
    dr = lambda n, s, d, k="ExternalInput": nc.dram_tensor(n, s, d, kind=k).ap()
    x_d = dr("x", [T, D], F32)
    pos_d = dr("pos", [128, NB], I32)
    mask_d = dr("mask", [128, 3, 512], BF16)
    hv_d = dr("hv", [128, 2], F32)
    ident_d = dr("ident", [128, 128], BF16)
    invf_d = dr("invf", [128, 32], F32)
    gains_d = dr("gains", [128, 3 * DEPTH, 8], F32)
    gfin_d = dr("gfin", [D], F32)
    cvec_d = dr("cvec", [128, DEPTH, 4, 34], F32)
    sinks_d = dr("sinks", [DEPTH, 8], F32)
    W = {n: dr(n, [DEPTH] + WSHAPES[n], F32) for n in WNAMES}
    out_d = dr("out", [NOWN * 128, D], F32, "ExternalOutput")

    P = Prog()
    st = contextlib.ExitStack()
    with st:
        sb = lambda n, s, d: st.enter_context(nc.sbuf_tensor(n, s, d))
        xs = sb("xs", [128, NB, D], F32)
        hTflat = sb("hT", [128, max(8 * T, 124 * 128)], BF16)
        hT = hTflat[:, 0:8 * T].rearrange("p (c t) -> p c t", c=8)
        warena = sb("warena", [128, 26624], BF16)
        arb = sb("arb", [128, 9088], BF16)
        hnt = sb("hnt", [128, 2, 1024], BF16)
        arf = sb("arf", [128, 3712], F32)
        cosT = sb("cosT", [128, NB, 32], F32)
        sinT = sb("sinT", [128, NB, 32], F32)
        maskS = sb("maskS", [128, 3, 512], BF16)
        ident = sb("ident_s", [128, 128], BF16)
        onesf = sb("onesf", [128, 128], F32)
        cvec = sb("cvec_s", [128, DEPTH, 4, 34], F32)
        gains = sb("gains_s", [128, 3 * DEPTH, 8], F32)
        hv = sb("hv_s", [128, 2], F32)
        invf = sb("invf_s", [128, 32], F32)
        posi = sb("posi", [128, NB], I32)
        posf = sb("posf", [128, NB], F32)
        ss = sb("ss", [128, NB], F32)
        rs = sb("rs", [128, NB], F32)
        esink = sb("esink", [128, 8], F32)
        den = sb("den", [128, 8], F32)
        vx = sb("vx", [128, 3, 2, 65], BF16)
        junk = sb("junk", [128, 2], F32)
        pfb = [st.enter_context(nc.psum_tensor("pf%d" % i, [128, 512], F32)) for i in range(6)]
        pbb = [st.enter_context(nc.psum_tensor("pb%d" % i, [128, 8, 128], BF16)) for i in range(2)]
        cnt = {"pf": 0, "pb": 0}

        def pf():
            i = cnt["pf"] % 6
            cnt["pf"] += 1
            return pfb[i], ("pf", i)

        def pb():
            i = cnt["pb"] % 2
            cnt["pb"] += 1
            return pbb[i], ("pb", i)

        AR = "ARENA"

        def vb(a, b, **kw):
            v = arb[:, a:b]
            if kw:
                pat = kw.pop("pat")
                v = v.rearrange(pat, **kw)
            return v

        def vf(a, b, **kw):
            v = arf[:, a:b]
            if kw:
                pat = kw.pop("pat")
                v = v.rearrange(pat, **kw)
            return v

        hn = [hnt[:, 0, :], hnt[:, 1, :]]
        act = [vb(0, 2048, pat="p (c t) -> p c t", c=4), vb(2048, 4096, pat="p (c t) -> p c t", c=4)]
        silu_t = [vf(0, 512), vf(512, 1024)]
        PT = vb(0, 2048, pat="p (g k t) -> p g k t", g=2, k=2)
        attnT = vb(2048, 3072, pat="p (c t) -> p c t", c=4)
        convo = vb(3072, 4096, pat="p (c t) -> p c t", c=4)
        q_r = vb(4096, 4608, pat="p (h d) -> p h d", h=8)
        q_r2 = vb(4096, 4608)
        k_r = vb(4608, 4736, pat="p (h d) -> p h d", h=2)
        attn_tok = vb(4736, 5248, pat="p (h d) -> p h d", h=8)
        attn_tok2 = vb(4736, 5248)
        qT = vb(5248, 6272)
        kT = [vb(6272, 6528, pat="p (g t) -> p g t", g=2), vb(6528, 6784, pat="p (g t) -> p g t", g=2)]
        hg2 = [vb(6784, 7928, pat="p (c t) -> p c t", c=4), vb(7928, 9072, pat="p (c t) -> p c t", c=4)]
        acc = vf(0, 1024, pat="p (c t) -> p c t", c=4)
        ysq = [vf(1024, 1280), vf(1280, 1536)]
        sig = [vf(1536, 1792), vf(1792, 2048)]
        mean = vf(2048, 2304)
        rstdb = vf(2304, 2560)
        var = vf(2560, 2816)
        kf = vf(2816, 2944, pat="p (h d) -> p h d", h=2)
        ropeA = vf(2944, 3200, pat="p (h d) -> p h d", h=8)
        ropeB = vf(3200, 3456, pat="p (h d) -> p h d", h=8)
        tmpc = [vf(3456, 3712)]
        gfin_v = vf(1024, 2048)
        dg = hTflat[:, 0:124 * 128].rearrange("p (i m) -> p i m", m=128)
        S0, S1, SX = 0, 12288, 24576
        wg_v = [warena[:, s:s + 4096].rearrange("p (c f) -> p c f", c=8) for s in (S0, S1)]
        wu_v = [warena[:, s + 4096:s + 8192].rearrange("p (c f) -> p c f", c=8) for s in (S0, S1)]
        wd_v = [warena[:, s + 8192:s + 12288].rearrange("p (c d) -> p c d", c=4) for s in (S0, S1)]
        win_q = warena[:, S0:S0 + 4096].rearrange("p (c f) -> p c f", c=8)
        win_u = warena[:, S0 + 4096:S0 + 12288].rearrange("p (c f) -> p c f", c=8)
        win_kv = warena[:, SX:SX + 2048].rearrange("p (c f) -> p c f", c=8)
        wout_v = warena[:, S1:S1 + 8192].rearrange("p (c f) -> p c f", c=8)
        hTg = [warena[:, S1 + 8192 + i * 2048:S1 + 8192 + (i + 1) * 2048].rearrange("p (c t) -> p c t", c=8) for i in range(2)]

        def barrier():
            P.add("vector", lambda e: e.memset(junk[:, 0:1], 0.0), writes=[AR, "junk"])

        P.add("sync", lambda e: e.dma_start(out=ident[:], in_=ident_d), writes=["ident"], dma=True)
        P.add("sync", lambda e: e.dma_start(out=posi[:], in_=pos_d), writes=["posi"], dma=True)
        P.add("sync", lambda e: e.dma_start(out=invf[:], in_=invf_d), writes=["invf"], dma=True)
        P.add("sync", lambda e: e.dma_start(out=gains[:], in_=gains_d), writes=["gains"], dma=True)
        for b in range(NB):
            P.add("sync", lambda e, b=b: e.dma_start(out=xs[:, b, :], in_=x_d[b * 128:(b + 1) * 128, :]),
                  writes=[("x", b)], dma=True)
        P.add("sync", lambda e: e.dma_start(out=maskS[:], in_=mask_d), writes=["mask"], dma=True)
        P.add("sync", lambda e: e.dma_start(out=hv[:], in_=hv_d), writes=["hv"], dma=True)
        P.add("sync", lambda e: e.dma_start(out=cvec[:], in_=cvec_d), writes=["cvec"], dma=True)
        P.add("vector", lambda e: e.memset(onesf[:], 1.0), writes=["onesf"])
        P.add("vector", lambda e: e.memset(vx[:], 1.0), writes=[("vx", 0), ("vx", 1), ("vx", 2)])

        P.add("vector", lambda e: e.tensor_copy(out=posf[:], in_=posi[:]), reads=["posi"], writes=["posf"])
        ub = arf[:, 0:NB * 32].rearrange("p (b f) -> p b f", f=32)
        ui = arf[:, 1024:1024 + NB * 32].bitcast(I32).rearrange("p (b f) -> p b f", f=32)
        uf = arf[:, 2048:2048 + NB * 32].rearrange("p (b f) -> p b f", f=32)
        P.add("vector", lambda e: e.tensor_tensor(out=ub, in0=posf[:].unsqueeze(2).to_broadcast([128, NB, 32]),
                                                  in1=invf[:].unsqueeze(1).to_broadcast([128, NB, 32]), op=ALU.mult),
              reads=["posf", "invf", AR], writes=["ub"])
        for name, tab, off in (("sin", sinT, 0.0), ("cos", cosT, 0.25)):
            if off != 0.0:
                P.add("vector", lambda e, off=off: e.tensor_scalar(ub, ub, off, None, ALU.add), reads=["ub", AR], writes=["ub"])
            P.add("vector", lambda e: e.tensor_copy(out=ui, in_=ub), reads=["ub", AR], writes=["ui"])
            P.add("vector", lambda e: e.tensor_copy(out=uf, in_=ui), reads=["ui", AR], writes=["uf"])
            P.add("vector", lambda e: e.tensor_tensor(out=uf, in0=ub, in1=uf, op=ALU.subtract), reads=["ub", "uf", AR], writes=["uf"])
            P.add("scalar", lambda e, tab=tab: e.activation(out=tab[:], in_=uf, func=AF.Sin, scale=2.0 * np.pi),
                  reads=["uf", AR], writes=[name])
        barrier()

        sqj = vb(4096, 5120)

        def norm_blocks(gi, blocks, dest=None):
            b0, b1 = blocks[0], blocks[-1] + 1
            assert list(blocks) == list(range(b0, b1))
            for b in blocks:
                jk, jt = (sqj, "sqj") if dest is None else (hn[b % 2], ("hn", b % 2))
                P.add("scalar", lambda e, b=b, jk=jk: e.activation(out=jk, in_=xs[:, b, :], func=AF.Square, accum_out=ss[:, b:b + 1]),
                      reads=[("x", b), AR], writes=[jt, ("ss", b)])
            sst = [("ss", b) for b in blocks]
            rst = [("rs", b) for b in blocks]
            P.add("vector", lambda e: e.tensor_scalar(rs[:, b0:b1], ss[:, b0:b1], 1.0 / D, EPS, ALU.mult, ALU.add), reads=sst, writes=rst)
            P.add("scalar", lambda e: e.activation(out=rs[:, b0:b1], in_=rs[:, b0:b1], func=AF.Sqrt), reads=rst, writes=rst)
            P.add("vector", lambda e: e.reciprocal(out=rs[:, b0:b1], in_=rs[:, b0:b1]), reads=rst, writes=rst)
            for i, b in enumerate(blocks):
                p = b % 2
                P.add("scalar", lambda e, b=b, p=p: e.activation(out=hn[p], in_=xs[:, b, :], func=AF.Copy, scale=rs[:, b:b + 1]),
                      reads=[("x", b), ("rs", b), AR], writes=[("hn", p)])
                bank, bt = pb()
                for c in range(8):
                    P.add("tensor", lambda e, c=c, p=p, bank=bank: e.transpose(bank[:, c, :], hn[p][:, c * 128:(c + 1) * 128], ident[:]),
                          reads=[("hn", p), "ident", AR], writes=[bt])
                if dest is None:
                    dap, dtok = hT[:, :, b * 128:(b + 1) * 128], ("hT", b)
                else:
                    dap, dtok = dest(i)
                P.add("vector", lambda e, bank=bank, dap=dap: e.tensor_tensor(
                    out=dap, in0=bank[:],
                    in1=gains[:, gi, :].unsqueeze(2).to_broadcast([128, 8, 128]), op=ALU.mult),
                    reads=[bt, "gains", AR], writes=[dtok])

        ffn_groups_all = [[0, 1]] + [list(range(2 + 4 * i, 2 + 4 * i + 4)) for i in range(NOWN // 4)]
        fgroups = [(0, 4), (4, 4), (8, 4), (12, 4), (16, 4), (20, 2)]
        slotctr = [0]

        def ffn(l, gname, uname, dname, post_norm=None, pre_norm=None, post_hook=None, first_block=0):
            ffn_groups = [[b for b in g if b >= first_block] for g in ffn_groups_all]
            ffn_groups = [g for g in ffn_groups if g]
            for fgi, (c0, gsz) in enumerate(fgroups):
                s = slotctr[0] % 2
                slotctr[0] += 1
                f0, f1 = c0 * 128, (c0 + gsz) * 128
                P.add("gpsimd", lambda e, s=s, f0=f0, f1=f1, gsz=gsz: e.dma_start(
                    out=wg_v[s][:, :, 0:gsz * 128], in_=W[gname][l, :, f0:f1].rearrange("(c p) f -> p c f", p=128)),
                    writes=[("slot", s)], dma=True)
                P.add("gpsimd", lambda e, s=s, f0=f0, f1=f1, gsz=gsz: e.dma_start(
                    out=wu_v[s][:, :, 0:gsz * 128], in_=W[uname][l, :, f0:f1].rearrange("(c p) f -> p c f", p=128)),
                    writes=[("slotu", s)], dma=True)
                P.add("gpsimd", lambda e, s=s, f0=f0, f1=f1, gsz=gsz: e.dma_start(
                    out=wd_v[s][:, 0:gsz, :], in_=W[dname][l, f0:f1, :].rearrange("(c p) d -> p c d", p=128)),
                    reads=([AR] if s == 1 else []), writes=[("slotd", s)], dma=True)

                def gu(ti, s=s, gsz=gsz):
                    blocks = ffn_groups[ti]
                    t0, Tg = blocks[0] * 128, len(blocks) * 128
                    p = ti % 2
                    hreads = [("hT", b) for b in blocks]
                    for j in range(gsz):
                        ba, ta = pf()
                        bu, tu = pf()
                        for c in range(8):
                            P.add("tensor", lambda e, c=c, j=j, ba=ba: e.matmul(
                                ba[:, 0:Tg], lhsT=wg_v[s][:, c, j * 128:(j + 1) * 128], rhs=hT[:, c, t0:t0 + Tg],
                                start=(c == 0), stop=(c == 7)), reads=hreads + [("slot", s), AR], writes=[ta])
                        for c in range(8):
                            P.add("tensor", lambda e, c=c, j=j, bu=bu: e.matmul(
                                bu[:, 0:Tg], lhsT=wu_v[s][:, c, j * 128:(j + 1) * 128], rhs=hT[:, c, t0:t0 + Tg],
                                start=(c == 0), stop=(c == 7)), reads=hreads + [("slotu", s), AR], writes=[tu])
                        sp = j % 2
                        P.add("scalar", lambda e, ba=ba, sp=sp: e.activation(out=silu_t[sp][:, 0:Tg], in_=ba[:, 0:Tg], func=AF.Silu),
                              reads=[ta, AR], writes=[("silu", sp)])
                        P.add("vector", lambda e, bu=bu, sp=sp, j=j, p=p: e.tensor_tensor(
                            out=act[p][:, j, 0:Tg], in0=bu[:, 0:Tg], in1=silu_t[sp][:, 0:Tg], op=ALU.mult),
                            reads=[tu, ("silu", sp), AR], writes=[("act", p, j)])

                def down(ti, s=s, gsz=gsz):
                    blocks = ffn_groups[ti]
                    p = ti % 2
                    for bl, b in enumerate(blocks):
                        for half in range(2):
                            bk, tk = pf()
                            for j in range(gsz):
                                P.add("tensor", lambda e, j=j, bl=bl, half=half, bk=bk: e.matmul(
                                    bk[:, :], lhsT=act[p][:, j, bl * 128:(bl + 1) * 128],
                                    rhs=wd_v[s][:, j, half * 512:(half + 1) * 512],
                                    start=(j == 0), stop=(j == gsz - 1)),
                                    reads=[("act", p, j), ("slotd", s), AR], writes=[tk])
                            P.add("vector", lambda e, b=b, half=half, bk=bk: e.scalar_tensor_tensor(
                                out=xs[:, b, half * 512:(half + 1) * 512], in0=bk[:, :], scalar=0.5,
                                in1=xs[:, b, half * 512:(half + 1) * 512], op0=ALU.mult, op1=ALU.add),
                                reads=[tk, ("x", b)], writes=[("x", b)])

                ng = len(ffn_groups)
                pn = pre_norm is not None and fgi == 0
                if pn:
                    norm_blocks(pre_norm, ffn_groups[0])
                    if ng > 1:
                        norm_blocks(pre_norm, ffn_groups[1])
                gu(0)
                for ti in range(ng):
                    if pn and ti + 2 < ng:
                        norm_blocks(pre_norm, ffn_groups[ti + 2])
                    if ti + 1 < ng:
                        gu(ti + 1)
                    down(ti)
                    if fgi == len(fgroups) - 1:
                        if post_norm is not None:
                            norm_blocks(post_norm, ffn_groups[ti])
                        if post_hook is not None:
                            post_hook(ffn_groups[ti])

        mix_groups = [[2 * i, 2 * i + 1] for i in range(NB // 2)]

        def mixer(l):
            P.add("gpsimd", lambda e: e.dma_start(out=win_q, in_=W["w_in"][l, :, 0:512].rearrange("(c p) f -> p c f", p=128)),
                  writes=[("slot", 0)], dma=True)
            P.add("gpsimd", lambda e: e.dma_start(out=win_kv, in_=W["w_in"][l, :, 512:768].rearrange("(c p) f -> p c f", p=128)),
                  writes=["slotx"], dma=True)
            P.add("gpsimd", lambda e: e.dma_start(out=win_u, in_=W["w_in"][l, :, 768:1792].rearrange("(c p) f -> p c f", p=128)),
                  writes=[("slotu", 0), ("slotd", 0)], dma=True)
            P.add("gpsimd", lambda e: e.dma_start(out=wout_v, in_=W["w_out"][l].rearrange("(c p) f -> p c f", p=128)),
                  writes=[("slot", 1), ("slotu", 1), ("slotd", 1)], dma=True)
            WQ, WU, WKV, WO = [("slot", 0)], [("slotu", 0), ("slotd", 0)], ["slotx"], [("slot", 1), ("slotu", 1), ("slotd", 1)]
            P.add("sync", lambda e: e.dma_start(out=esink[:], in_=sinks_d[l].partition_broadcast(128)), writes=["esink"], dma=True)
            P.add("scalar", lambda e: e.activation(out=esink[:], in_=esink[:], func=AF.Exp), reads=["esink"], writes=["esink"])
            cv = lambda c, k: cvec[:, l, c, k:k + 1]
            dg_built = set()
            P.add("vector", lambda e: e.memset(qT[64:128, :], 0.0), reads=[AR], writes=["qT"])
            for i_ in range(2):
                P.add("vector", lambda e, i_=i_: e.memset(kT[i_][64:128, :, :], 0.0), reads=[AR], writes=[("kT", i_)])
            def group_gen(gi, blocks):
                gp = gi % 2
                hg, hgo = hg2[gp], hg2[1 - gp]
                light = (l == DEPTH - 1 and gi == 0)
                norm_blocks(3 * l + 1, blocks, dest=lambda i, gp=gp: (hTg[gp][:, :, i * 128:(i + 1) * 128], ("hTg", gp, i)))
                yield
                hreads = [("hTg", gp, 0), ("hTg", gp, 1), AR]
                if gi == 0:
                    P.add("vector", lambda e: e.memset(hg[:, :, 0:30], 0.0), reads=[AR], writes=[("hg", gp)])
                else:
                    P.add("vector", lambda e: e.tensor_copy(out=hg[:, :, 0:30], in_=hgo[:, :, 256:286]), reads=[AR, ("hg", 1 - gp)], writes=[("hg", gp)])
                hvc = hv[:, 0:1] if gi == 0 else hv[:, 1:2]
                for c in range(4):
                    bk, tk = pf()
                    for kc in range(8):
                        P.add("tensor", lambda e, c=c, kc=kc, bk=bk, gp=gp: e.matmul(
                            bk[:, 0:256], lhsT=win_u[:, kc, c * 128:(c + 1) * 128], rhs=hTg[gp][:, kc, :],
                            start=(kc == 0), stop=(kc == 7)), reads=hreads + WU, writes=[tk])
                    for kc in range(8):
                        P.add("tensor", lambda e, c=c, kc=kc, bk=bk, gp=gp: e.matmul(
                            bk[:, 256:512], lhsT=win_u[:, kc, 512 + c * 128:512 + (c + 1) * 128], rhs=hTg[gp][:, kc, :],
                            start=(kc == 0), stop=(kc == 7)), reads=hreads + WU, writes=[tk])
                    sp = c % 2
                    P.add("scalar", lambda e, bk=bk, sp=sp: e.activation(out=sig[sp], in_=bk[:, 256:512], func=AF.Sigmoid),
                          reads=[tk, AR], writes=[("sig", sp)])
                    P.add("vector", lambda e, bk=bk, sp=sp, c=c, hvc=hvc: e.scalar_tensor_tensor(
                        out=hg[:, c, 30:286], in0=bk[:, 0:256], scalar=hvc, in1=sig[sp], op0=ALU.mult, op1=ALU.mult),
                        reads=[tk, ("sig", sp), "hv", AR, ("hg", gp)], writes=[("hg", gp), ("hgc", gp, c)])
                yield
                conv_pieces = []
                cbank = {}

                def conv_piece(c):
                    if c not in dg_built:
                        dg_built.add(c)
                        for k in range(CW):
                            P.add("vector", lambda e, k=k: e.tensor_scalar(dg[:, c * CW + k, :], ident[:], cv(c, k), None, ALU.mult),
                                  reads=["ident", "cvec", AR], writes=[("dg", c)])
                    cp, ch = divmod(c, 2)
                    if ch == 0:
                        cbank[cp] = pf()
                    bkc, tkc = cbank[cp]
                    for k in range(CW):
                        P.add("tensor", lambda e, k=k: e.matmul(
                            bkc[:, ch * 256:(ch + 1) * 256], lhsT=dg[:, c * CW + k, :], rhs=hg[:, c, k:k + 256],
                            start=(k == 0), stop=(k == CW - 1)),
                            reads=[("dg", c), ("hgc", gp, c), ("hg", gp), AR], writes=[tkc])
                    if ch == 1:
                        for ch2 in range(2):
                            c2 = 2 * cp + ch2
                            P.add("scalar", lambda e, c2=c2, ch2=ch2: e.activation(
                                out=acc[:, c2, :], in_=bkc[:, ch2 * 256:(ch2 + 1) * 256], func=AF.Identity, bias=cv(c2, 31)),
                                reads=[tkc, "cvec", AR], writes=[("acc", c2)])
                if not light:
                    for c in range(4):
                        conv_pieces.append(lambda c=c: conv_piece(c))
                def block_gen(bl, b):
                    if light and b == 0:
                        return
                    par = b % 2
                    vs, vsp = b % 3, (b - 1) % 3
                    qp, tq = pf()
                    for kc in range(8):
                        P.add("tensor", lambda e, kc=kc: e.matmul(
                            qp[:, :], lhsT=hTg[gp][:, kc, bl * 128:(bl + 1) * 128], rhs=win_q[:, kc, :],
                            start=(kc == 0), stop=(kc == 7)), reads=[("hTg", gp, bl), AR] + WQ, writes=[tq])
                    kvp, tkv = pf()
                    for kc in range(8):
                        P.add("tensor", lambda e, kc=kc: e.matmul(
                            kvp[:, 0:256], lhsT=hTg[gp][:, kc, bl * 128:(bl + 1) * 128], rhs=win_kv[:, kc, :],
                            start=(kc == 0), stop=(kc == 7)), reads=[("hTg", gp, bl), AR] + WKV, writes=[tkv])
                    P.add("scalar", lambda e: e.copy(
                        out=vx[:, vs, :, 0:64], in_=kvp[:, 128:256].rearrange("p (g d) -> p g d", g=2)),
                        reads=[tkv], writes=[("vx", vs)])
                    yield
                    cb = cosT[:, b, :].unsqueeze(1).to_broadcast([128, 8, 32])
                    sbq = sinT[:, b, :].unsqueeze(1).to_broadcast([128, 8, 32])
                    cb2 = cosT[:, b, :].unsqueeze(1).to_broadcast([128, 2, 32])
                    sb2 = sinT[:, b, :].unsqueeze(1).to_broadcast([128, 2, 32])
                    q3 = qp[:, :].rearrange("p (h d) -> p h d", h=8)
                    k3 = kvp[:, 0:128].rearrange("p (h d) -> p h d", h=2)
                    TT = lambda out, a, bb, op, reads, writes: P.add(
                        "vector", lambda e: e.tensor_tensor(out=out, in0=a, in1=bb, op=op), reads=reads + [AR], writes=writes)
                    ropeC = ysq[0].rearrange("p (h d) -> p h d", h=8)
                    ropeD = ysq[1].rearrange("p (h d) -> p h d", h=8)
                    YC, YD = ("ysq", 0), ("ysq", 1)
                    TT(ropeA, q3[:, :, 0:32], cb, ALU.mult, [tq, "cos"], ["ropeA"])
                    TT(ropeB, q3[:, :, 32:64], sbq, ALU.mult, [tq, "sin"], ["ropeB"])
                    TT(ropeC, q3[:, :, 32:64], cb, ALU.mult, [tq, "cos"], [YC])
                    TT(ropeD, q3[:, :, 0:32], sbq, ALU.mult, [tq, "sin"], [YD])
                    TT(q_r[:, :, 0:32], ropeA, ropeB, ALU.subtract, ["ropeA", "ropeB"], [("q_r", 0)])
                    TT(q_r[:, :, 32:64], ropeC, ropeD, ALU.add, [YC, YD], [("q_r", 1)])
                    rA2, rB2, rC2, rD2 = ropeA[:, 0:2, :], ropeB[:, 0:2, :], ropeC[:, 0:2, :], ropeD[:, 0:2, :]
                    TT(rA2, k3[:, :, 0:32], cb2, ALU.mult, [tkv, "cos"], ["ropeA"])
                    TT(rB2, k3[:, :, 32:64], sb2, ALU.mult, [tkv, "sin"], ["ropeB"])
                    TT(rC2, k3[:, :, 32:64], cb2, ALU.mult, [tkv, "cos"], [YC])
                    TT(rD2, k3[:, :, 0:32], sb2, ALU.mult, [tkv, "sin"], [YD])
                    TT(k_r[:, :, 0:32], rA2, rB2, ALU.subtract, ["ropeA", "ropeB"], [("k_r", 0)])
                    TT(k_r[:, :, 32:64], rC2, rD2, ALU.add, [YC, YD], [("k_r", 1)])
                    yield
                    pq, tpq = pb()
                    for h in range(8):
                        P.add("tensor", lambda e, h=h: e.transpose(pq[0:64, h, :], q_r[:, h, :], ident[:]),
                              reads=[("q_r", 0), ("q_r", 1), "ident", AR], writes=[tpq])
                    P.add("scalar", lambda e: e.copy(out=qT[0:64, :], in_=pq[0:64, :, :].rearrange("p h t -> p (h t)")),
                          reads=[tpq, AR], writes=["qT"])
                    pk, tpk = pb()
                    for g in range(2):
                        P.add("tensor", lambda e, g=g: e.transpose(pk[0:64, g, :], k_r[:, g, :], ident[:]),
                              reads=[("k_r", 0), ("k_r", 1), "ident", AR], writes=[tpk])
                    P.add("scalar", lambda e: e.copy(out=kT[par][0:64, :, :], in_=pk[0:64, 0:2, :]),
                          reads=[tpk, AR], writes=[("kT", par)])
                    if light:
                        return
                    yield
                    kbs = ([] if b == 0 else [(0, 1 - par, 2 if b == 2 else 1, vsp)]) + [(1, par, 0, vs)]
                    for g in range(2):
                        for (ki, kpar, mi, vsl) in kbs:
                            stb, tst = pf()
                            P.add("tensor", lambda e, g=g, kpar=kpar, stb=stb: e.matmul(
                                stb[:, :], lhsT=kT[kpar][:, g, :], rhs=qT[:, g * 512:(g + 1) * 512],
                                start=True, stop=False), reads=[("kT", kpar), "qT", AR], writes=[tst])
                            P.add("tensor", lambda e, mi=mi, stb=stb: e.matmul(
                                stb[:, :], lhsT=ident[:], rhs=maskS[:, mi, :], start=False, stop=True),
                                reads=["ident", "mask"], writes=[tst])
                            P.add("scalar", lambda e, g=g, ki=ki, stb=stb: e.activation(
                                out=PT[:, g, ki, :], in_=stb[:, :], func=AF.Exp, scale=0.125),
                                reads=[tst, AR], writes=[("PT", g, ki)])
                    yield
                    ovs = []
                    for g in range(2):
                        ov, tov = pf()
                        ovs.append((ov, tov))
                        for hl in range(4):
                            for n, (ki, kpar, mi, vsl) in enumerate(kbs):
                                P.add("tensor", lambda e, g=g, hl=hl, ki=ki, vsl=vsl, ov=ov, n=n, last=(n == len(kbs) - 1): e.matmul(
                                    ov[:, hl * 65:(hl + 1) * 65], lhsT=PT[:, g, ki, hl * 128:(hl + 1) * 128],
                                    rhs=vx[:, vsl, g, :], start=(n == 0), stop=last),
                                    reads=[("PT", g, ki), ("vx", vsl), AR], writes=[tov])
                        ov3 = ov[:, 0:260].rearrange("p (h d) -> p h d", h=4)
                        P.add("vector", lambda e, g=g, ov3=ov3: e.tensor_tensor(
                            out=den[:, 4 * g:4 * g + 4], in0=ov3[:, :, 64], in1=esink[:, 4 * g:4 * g + 4], op=ALU.add),
                            reads=[tov, "esink"], writes=[("den", g)])
                    P.add("vector", lambda e: e.reciprocal(out=den[:], in_=den[:]), reads=[("den", 0), ("den", 1)],
                          writes=[("den", 0), ("den", 1)])
                    for g in range(2):
                        ov, tov = ovs[g]
                        ov3 = ov[:, 0:260].rearrange("p (h d) -> p h d", h=4)
                        P.add("vector", lambda e, g=g, ov3=ov3: e.tensor_tensor(
                            out=attn_tok[:, 4 * g:4 * g + 4, :], in0=ov3[:, :, 0:64],
                            in1=den[:, 4 * g:4 * g + 4].unsqueeze(2).to_broadcast([128, 4, 64]), op=ALU.mult),
                            reads=[tov, ("den", g), AR], writes=[("atok", g)])
                    yield
                    pa, tpa = pb()
                    for c in range(4):
                        P.add("tensor", lambda e, c=c: e.transpose(pa[:, c, :], attn_tok2[:, c * 128:(c + 1) * 128], ident[:]),
                              reads=[("atok", 0), ("atok", 1), "ident", AR], writes=[tpa])
                    P.add("scalar", lambda e: e.copy(out=attnT[:, :, bl * 128:(bl + 1) * 128], in_=pa[:, 0:4, :]),
                          reads=[tpa, AR], writes=[("attnT", bl)])

                bg = [block_gen(bl, b) for bl, b in enumerate(blocks)]

                def step(i):
                    try:
                        next(bg[i])
                    except StopIteration:
                        pass

                def cpiece():
                    if conv_pieces:
                        conv_pieces.pop(0)()
                step(0)
                yield
                for item in (1, 0, 0, "c", 1, 0, 1, "c", 0, 1, "c", 0, 1, "c", 1, 0, 1):
                    if item == "c":
                        cpiece()
                    else:
                        step(item)
                if light:
                    yield
                    return
                while conv_pieces:
                    conv_pieces.pop(0)()
                b1, t1 = pf()
                b2, t2 = pf()
                for c in range(4):
                    sp = c % 2
                    P.add("scalar", lambda e, c=c, sp=sp: e.activation(out=ysq[sp], in_=acc[:, c, :], func=AF.Square),
                          reads=[("acc", c), AR], writes=[("ysq", sp)])
                    P.add("tensor", lambda e, c=c, b1=b1: e.matmul(b1[:, 0:256], lhsT=onesf[:], rhs=acc[:, c, :], start=(c == 0), stop=(c == 3)),
                          reads=[("acc", c), "onesf", AR], writes=[t1])
                    P.add("tensor", lambda e, c=c, sp=sp, b2=b2: e.matmul(b2[:, 0:256], lhsT=onesf[:], rhs=ysq[sp], start=(c == 0), stop=(c == 3)),
                          reads=[("ysq", sp), "onesf", AR], writes=[t2])
                yield
                P.add("vector", lambda e, b1=b1: e.tensor_scalar(mean, b1[:, 0:256], 1.0 / 512, None, ALU.mult), reads=[t1, AR], writes=["mean"])
                P.add("vector", lambda e: e.tensor_tensor(out=var, in0=mean, in1=mean, op=ALU.mult), reads=["mean", AR], writes=["var"])
                P.add("vector", lambda e, b2=b2: e.scalar_tensor_tensor(out=var, in0=b2[:, 0:256], scalar=1.0 / 512, in1=var,
                                                                 op0=ALU.mult, op1=ALU.subtract), reads=[t2, "var", AR], writes=["var"])
                P.add("vector", lambda e: e.tensor_scalar(var, var, EPS, None, ALU.add), reads=["var", AR], writes=["var"])
                P.add("scalar", lambda e: e.activation(out=var, in_=var, func=AF.Sqrt), reads=["var", AR], writes=["var"])
                P.add("vector", lambda e: e.reciprocal(out=rstdb, in_=var), reads=["var", AR], writes=["rstdb"])
                for c in range(4):
                    tp, tpt = (tmpc[0], "tmpc0") if c % 2 == 0 else (vf(2944, 3200), "ropeA")
                    P.add("vector", lambda e, c=c, tp=tp: e.tensor_tensor(out=tp, in0=acc[:, c, :], in1=mean, op=ALU.subtract),
                          reads=[("acc", c), "mean", AR], writes=[tpt])
                    P.add("vector", lambda e, tp=tp: e.tensor_tensor(out=tp, in0=tp, in1=rstdb, op=ALU.mult),
                          reads=[tpt, "rstdb", AR], writes=[tpt])
                    P.add("scalar", lambda e, c=c, tp=tp: e.activation(out=convo[:, c, :], in_=tp, func=AF.Silu,
                                                                        scale=cv(c, 32), bias=cv(c, 33)),
                          reads=[tpt, "cvec", AR], writes=[("convo", c)])
                xos = {}
                for bl, b in enumerate(blocks):
                    for half in range(2):
                        xo, txo = pf()
                        xos[(bl, half)] = (xo, txo)
                        for c in range(4):
                            P.add("tensor", lambda e, c=c, bl=bl, half=half, xo=xo: e.matmul(
                                xo[:, :], lhsT=attnT[:, c, bl * 128:(bl + 1) * 128], rhs=wout_v[:, c, half * 512:(half + 1) * 512],
                                start=(c == 0), stop=False), reads=[("attnT", bl), AR] + WO, writes=[txo])
                yield
                for bl, b in enumerate(blocks):
                    for half in range(2):
                        xo, txo = xos[(bl, half)]
                        for c in range(4):
                            P.add("tensor", lambda e, c=c, bl=bl, half=half, xo=xo: e.matmul(
                                xo[:, :], lhsT=convo[:, c, bl * 128:(bl + 1) * 128], rhs=wout_v[:, 4 + c, half * 512:(half + 1) * 512],
                                start=False, stop=(c == 3)), reads=[("convo", c), AR] + WO, writes=[txo])
                        P.add("vector", lambda e, b=b, half=half, xo=xo: e.tensor_tensor(
                            out=xs[:, b, half * 512:(half + 1) * 512], in0=xo[:, :], in1=xs[:, b, half * 512:(half + 1) * 512], op=ALU.add),
                            reads=[txo, ("x", b)], writes=[("x", b)])

            gens = [group_gen(gi, blocks) for gi, blocks in enumerate(mix_groups)]

            def adv(i):
                if i < len(gens):
                    try:
                        next(gens[i])
                    except StopIteration:
                        pass
            adv(0)
            adv(0)
            adv(0)
            for gi in range(len(gens)):
                adv(gi + 1)
                adv(gi)
                adv(gi + 1)
                adv(gi)
                adv(gi + 1)
                for _ in gens[gi]:
                    pass

        gfin_loaded = [False]

        def final_out(blocks):
            if not gfin_loaded[0]:
                gfin_loaded[0] = True
                P.add("sync", lambda e: e.dma_start(out=gfin_v, in_=gfin_d.partition_broadcast(128)), reads=[AR], writes=["gfin"], dma=True)
            for b in blocks:
                if b < 2:
                    continue
                i = b - 2
                P.add("scalar", lambda e, b=b: e.activation(out=hn[0], in_=xs[:, b, :], func=AF.Square, accum_out=ss[:, b:b + 1]),
                      reads=[("x", b), AR], writes=[("hn", 0), ("ss", b)])
                P.add("vector", lambda e, b=b: e.tensor_scalar(rs[:, b:b + 1], ss[:, b:b + 1], 1.0 / D, EPS, ALU.mult, ALU.add),
                      reads=[("ss", b)], writes=[("rs", b)])
                P.add("scalar", lambda e, b=b: e.activation(out=rs[:, b:b + 1], in_=rs[:, b:b + 1], func=AF.Sqrt),
                      reads=[("rs", b)], writes=[("rs", b)])
                P.add("vector", lambda e, b=b: e.reciprocal(out=rs[:, b:b + 1], in_=rs[:, b:b + 1]),
                      reads=[("rs", b)], writes=[("rs", b)])
                P.add("vector", lambda e, b=b: e.scalar_tensor_tensor(out=xs[:, b, :], in0=xs[:, b, :], scalar=rs[:, b:b + 1],
                                                                      in1=gfin_v, op0=ALU.mult, op1=ALU.mult),
                      reads=[("x", b), ("rs", b), "gfin", AR], writes=[("x", b)])
                P.add("sync", lambda e, i=i, b=b: e.dma_start(out=out_d[i * 128:(i + 1) * 128, :], in_=xs[:, b, :]),
                      reads=[("x", b)], dma=True)

        allb = list(range(NB))
        stage = [0]

        def done():
            stage[0] += 1
            return stop_after is not None and stage[0] >= stop_after

        finished = False
        import os as _os
        KS = int(_os.environ.get("KSTOP", "0"))
        for l in range(DEPTH):
            if KS == 1:
                finished = True
                break
            if KS == 2:
                finished = True
                break
            ffn(l, "ffn1_w_gate", "ffn1_w_up", "ffn1_w_down", pre_norm=(0 if l == 0 else None), first_block=(0 if l == 0 else 1))
            if done():
                finished = True
                break
            barrier()
            mixer(l)
            barrier()
            if done():
                finished = True
                break
            last = (l + 1 == DEPTH) and stop_after is None
            ffn(l, "ffn2_w_gate", "ffn2_w_up", "ffn2_w_down", post_norm=(3 * (l + 1) if l + 1 < DEPTH else None),
                pre_norm=3 * l + 2, post_hook=(final_out if last else None), first_block=(2 if l + 1 == DEPTH else 1))
            if done():
                finished = True
                break
        barrier()
        if finished:
            for i in range(NOWN):
                b = i + 2
                P.add("sync", lambda e, i=i, b=b: e.dma_start(out=out_d[i * 128:(i + 1) * 128, :], in_=xs[:, b, :]),
                      reads=[("x", b)], dma=True)
        else:
            pass
        P.emit(nc)
    return nc


def make_inputs(inputs, n_cores, cores_per_seq, NOWN, DEPTH):
    x = np.asarray(inputs["x"], np.float32)
    pos = np.asarray(inputs["positions"], np.int32)
    B, S, _ = x.shape
    own = NOWN * 128
    NB = NOWN + 2
    bf = ml_dtypes.bfloat16
    i = np.arange(128)[None, :]
    j = np.arange(128)[:, None]
    m_cur = np.where(i >= j, 0.0, NEG).astype(np.float32)
    m_prev = np.where(j > i, 0.0, NEG).astype(np.float32)
    m_none = np.full((128, 128), NEG, np.float32)
    inv_freq = (1.0 / (10000.0 ** (np.arange(0, 64, 2, dtype=np.float32) / 64))).astype(np.float32)
    invf = np.broadcast_to((inv_freq.astype(np.float64) / (2 * np.pi)).astype(np.float32)[None, :], (128, 32)).copy()
    gains = np.zeros((128, 3 * DEPTH, 8), np.float32)
    for l in range(DEPTH):
        for k, n in enumerate(["ffn1_norm", "mix_norm", "ffn2_norm"]):
            gains[:, 3 * l + k, :] = np.asarray(inputs[n], np.float32)[l].reshape(8, 128).T
    cvec = np.zeros((128, DEPTH, 4, 34), np.float32)
    for l in range(DEPTH):
        cw = np.asarray(inputs["conv_w"], np.float32)[l]
        cvec[:, l, :, 0:31] = cw.T.reshape(4, 128, 31).transpose(1, 0, 2)
        for k, n in ((31, "conv_b"), (32, "conv_ln_g"), (33, "conv_ln_b")):
            cvec[:, l, :, k] = np.asarray(inputs[n], np.float32)[l].reshape(4, 128).T
    shared = {
        "ident": np.eye(128).astype(bf), "invf": invf, "gains": gains, "cvec": cvec,
        "gfin": np.asarray(inputs["final_norm"], np.float32),
        "sinks": np.asarray(inputs["attn_sinks"], np.float32),
    }
    for n in WNAMES:
        shared[n] = np.ascontiguousarray(np.asarray(inputs[n], np.float32))
    in_maps = []
    for core in range(n_cores):
        bi, ci = divmod(core, cores_per_seq)
        t0 = ci * own
        xc = np.zeros((NB * 128, D), np.float32)
        pc = np.zeros((NB * 128,), np.int32)
        lo = t0 - 256
        if lo >= 0:
            xc[:] = x[bi, lo:t0 + own]
            pc[:] = pos[bi, lo:t0 + own]
            start = False
        else:
            xc[256:] = x[bi, t0:t0 + own]
            pc[256:] = pos[bi, t0:t0 + own]
            start = True
        mask = np.stack([m_cur, m_prev, m_none if start else m_prev], axis=1)
        mask = np.repeat(mask[:, :, None, :], 4, axis=2).reshape(128, 3, 512).astype(bf)
        hvv = np.ones((128, 2), np.float32)
        hvv[:, 0] = 0.0 if start else 1.0
        m = dict(shared)
        m.update({"x": xc, "pos": np.ascontiguousarray(pc.reshape(NB, 128).T), "mask": mask, "hv": hvv})
        in_maps.append(m)
    return in_maps


def run(inputs, n_cores, cores_per_seq, NOWN, DEPTH=2, stop_after=None, trace=False):
    nc = build(NOWN, DEPTH, stop_after)
    in_maps = make_inputs(inputs, n_cores, cores_per_seq, NOWN, DEPTH)
    res = run_bass_kernel_spmd(nc, in_maps, core_ids=list(range(n_cores)), trace=trace)
    x = np.asarray(inputs["x"])
    B, S, _ = x.shape
    own = NOWN * 128
    out = np.zeros((B, S, D), np.float32)
    for core in range(n_cores):
        bi, ci = divmod(core, cores_per_seq)
        out[bi, ci * own:(ci + 1) * own] = res.results[core]["out"]
    return out, res


def kernel(**inputs):
    out, _ = run(inputs, 8, 4, 16, 2)
    return out
```

```python
import contextlib
import numpy as np
import ml_dtypes
import concourse.bass as bass
import concourse.mybir as mybir
from concourse.bass_utils import run_bass_kernel_spmd

F32 = mybir.dt.float32
BF16 = mybir.dt.bfloat16
I32 = mybir.dt.int32
AF = mybir.ActivationFunctionType
ALU = mybir.AluOpType

ENGINES = ("tensor", "vector", "scalar", "gpsimd", "sync")
NDMA_SEMS = 6
import os as _os0
NDMA_Q = {"gpsimd": 1, "sync": int(_os0.environ.get("NDSY", "2"))}

D = 1024
DFF = 2816
NFC = DFF // 128
DIN = 1792
CW = 31
EPS = 1e-5
NEG = -30000.0


class Op:
    __slots__ = ("eng", "fn", "waits", "signal", "dma", "idx")

    def __init__(self, eng, fn, dma=None):
        self.eng = eng
        self.fn = fn
        self.waits = {}
        self.signal = False
        self.dma = dma
        self.idx = None


class Prog:
    def __init__(self, same_engine_sync=("vector", "scalar", "gpsimd")):
        self.ops = {e: [] for e in ENGINES}
        self.last_w = {}
        self.readers = {}
        self.ndma = {e: 0 for e in ENGINES}
        self.same = set(same_engine_sync)

    def _add_dep(self, op, key, val):
        if key[0] == "e" and key[1] == op.eng and key[1] not in self.same:
            return
        if op.waits.get(key, -1) < val:
            op.waits[key] = val

    def add(self, eng, fn, reads=(), writes=(), dma=False):
        if dma:
            n = self.ndma[eng]
            self.ndma[eng] += 1
            nq = NDMA_Q.get(eng, NDMA_SEMS)
            k = n % nq
            val = 16 * (n // nq + 1)
            op = Op(eng, fn, dma=(eng, k, val))
            if val > 16:
                op.waits[("d", eng, k)] = val - 16
            mykey, myval = ("d", eng, k), val
        else:
            op = Op(eng, fn)
            mykey, myval = ("e", eng), len(self.ops[eng])
        op.idx = len(self.ops[eng])
        for t in reads:
            w = self.last_w.get(t)
            if w is not None:
                self._add_dep(op, w[0], w[1])
        for t in writes:
            w = self.last_w.get(t)
            if w is not None:
                self._add_dep(op, w[0], w[1])
            for k2, v2 in self.readers.get(t, {}).items():
                self._add_dep(op, k2, v2)
        for t in reads:
            r = self.readers.setdefault(t, {})
            if r.get(mykey, -1) < myval:
                r[mykey] = myval
        for t in writes:
            self.last_w[t] = (mykey, myval)
            self.readers[t] = {}
        self.ops[eng].append(op)
        return op

    def emit(self, nc):
        for e in ENGINES:
            for op in self.ops[e]:
                for key, v in op.waits.items():
                    if key[0] == "e":
                        self.ops[key[1]][v].signal = True
        counts = {}
        for e in ENGINES:
            c = 0
            arr = []
            for op in self.ops[e]:
                if op.signal:
                    c += 1
                arr.append(c)
            counts[e] = arr
        with contextlib.ExitStack() as st:
            esem = {e: st.enter_context(nc.semaphore("s_" + e)) for e in ENGINES}
            dsem = {}
            for e in ENGINES:
                if self.ndma[e]:
                    for k in range(NDMA_SEMS):
                        dsem[(e, k)] = st.enter_context(nc.semaphore("d_%s%d" % (e, k)))
            block = st.enter_context(nc.Block())

            def make(e):
                def body(eng):
                    waited = {}
                    for op in self.ops[e]:
                        for key, v in op.waits.items():
                            if key[0] == "e":
                                sem = esem[key[1]]
                                val = counts[key[1]][v]
                            else:
                                sem = dsem[(key[1], key[2])]
                                val = v
                            if waited.get(sem.name, -1) >= val:
                                continue
                            waited[sem.name] = val
                            eng.wait_ge(sem, val)
                        ins = op.fn(eng)
                        if op.dma is not None:
                            ins.then_inc(dsem[(op.dma[0], op.dma[1])], 16)
                        elif op.signal:
                            ins.then_inc(esem[e], 1)
                    if self.ndma[e]:
                        n = self.ndma[e]
                        nq = NDMA_Q.get(e, NDMA_SEMS)
                        for k in range(nq):
                            cnt = (n - k + nq - 1) // nq
                            if cnt > 0:
                                eng.wait_ge(dsem[(e, k)], 16 * cnt)
                return body

            for e in ENGINES:
                if self.ops[e]:
                    getattr(block, e)(make(e))


WNAMES = ["ffn1_w_gate", "ffn1_w_up", "ffn1_w_down", "w_in", "w_out",
          "ffn2_w_gate", "ffn2_w_up", "ffn2_w_down"]
WSHAPES = {"ffn1_w_gate": [D, DFF], "ffn1_w_up": [D, DFF], "ffn1_w_down": [DFF, D],
           "w_in": [D, DIN], "w_out": [D, D],
           "ffn2_w_gate": [D, DFF], "ffn2_w_up": [D, DFF], "ffn2_w_down": [DFF, D]}


def build(NOWN, DEPTH=2, stop_after=None):
    NB = NOWN + 2
    T = NB * 128
    nc = bass.Bass("TRN2", target_bir_lowering=False)
    dr = lambda n, s, d, k="ExternalInput": nc.dram_tensor(n, s, d, kind=k).ap()
    x_d = dr("x", [T, D], F32)
    pos_d = dr("pos", [128, NB], I32)
    mask_d = dr("mask", [128, 3, 512], BF16)
    hv_d = dr("hv", [128, 2], F32)
    ident_d = dr("ident", [128, 128], BF16)
    invf_d = dr("invf", [128, 32], F32)
    gains_d = dr("gains", [128, 3 * DEPTH, 8], F32)
    gfin_d = dr("gfin", [D], F32)
    cvec_d = dr("cvec", [128, DEPTH, 4, 34], F32)
    sinks_d = dr("sinks", [DEPTH, 8], F32)
    W = {n: dr(n, [DEPTH] + WSHAPES[n], F32) for n in WNAMES}
    out_d = dr("out", [NOWN * 128, D], F32, "ExternalOutput")

    P = Prog()
    st = contextlib.ExitStack()
    with st:
        sb = lambda n, s, d: st.enter_context(nc.sbuf_tensor(n, s, d))
        xs = sb("xs", [128, NB, D], F32)
        hTflat = sb("hT", [128, max(8 * T, 124 * 128)], BF16)
        hT = hTflat[:, 0:8 * T].rearrange("p (c t) -> p c t", c=8)
        warena = sb("warena", [128, 26624], BF16)
        arb = sb("arb", [128, 9088], BF16)
        hnt = sb("hnt", [128, 2, 1024], BF16)
        arf = sb("arf", [128, 3712], F32)
        cosT = sb("cosT", [128, NB, 32], F32)
        sinT = sb("sinT", [128, NB, 32], F32)
        maskS = sb("maskS", [128, 3, 512], BF16)
        ident = sb("ident_s", [128, 128], BF16)
        onesf = sb("onesf", [128, 128], F32)
        cvec = sb("cvec_s", [128, DEPTH, 4, 34], F32)
        gains = sb("gains_s", [128, 3 * DEPTH, 8], F32)
        hv = sb("hv_s", [128, 2], F32)
        invf = sb("invf_s", [128, 32], F32)
        posi = sb("posi", [128, NB], I32)
        posf = sb("posf", [128, NB], F32)
        ss = sb("ss", [128, NB], F32)
        rs = sb("rs", [128, NB], F32)
        esink = sb("esink", [128, 8], F32)
        den = sb("den", [128, 8], F32)
        vx = sb("vx", [128, 3, 2, 65], BF16)
        junk = sb("junk", [128, 2], F32)
        pfb = [st.enter_context(nc.psum_tensor("pf%d" % i, [128, 512], F32)) for i in range(6)]
        pbb = [st.enter_context(nc.psum_tensor("pb%d" % i, [128, 8, 128], BF16)) for i in range(2)]
        cnt = {"pf": 0, "pb": 0}

        def pf():
            i = cnt["pf"] % 6
            cnt["pf"] += 1
            return pfb[i], ("pf", i)

        def pb():
            i = cnt["pb"] % 2
            cnt["pb"] += 1
            return pbb[i], ("pb", i)

        AR = "ARENA"

        def vb(a, b, **kw):
            v = arb[:, a:b]
            if kw:
                pat = kw.pop("pat")
                v = v.rearrange(pat, **kw)
            return v

        def vf(a, b, **kw):
            v = arf[:, a:b]
            if kw:
                pat = kw.pop("pat")
                v = v.rearrange(pat, **kw)
            return v

        hn = [hnt[:, 0, :], hnt[:, 1, :]]
        act = [vb(0, 2048, pat="p (c t) -> p c t", c=4), vb(2048, 4096, pat="p (c t) -> p c t", c=4)]
        silu_t = [vf(0, 512), vf(512, 1024)]
        PT = vb(0, 2048, pat="p (g k t) -> p g k t", g=2, k=2)
        attnT = vb(2048, 3072, pat="p (c t) -> p c t", c=4)
        convo = vb(3072, 4096, pat="p (c t) -> p c t", c=4)
        q_r = vb(4096, 4608, pat="p (h d) -> p h d", h=8)
        q_r2 = vb(4096, 4608)
        k_r = vb(4608, 4736, pat="p (h d) -> p h d", h=2)
        attn_tok = vb(4736, 5248, pat="p (h d) -> p h d", h=8)
        attn_tok2 = vb(4736, 5248)
        qT = vb(5248, 6272)
        kT = [vb(6272, 6528, pat="p (g t) -> p g t", g=2), vb(6528, 6784, pat="p (g t) -> p g t", g=2)]
        hg2 = [vb(6784, 7928, pat="p (c t) -> p c t", c=4), vb(7928, 9072, pat="p (c t) -> p c t", c=4)]
        acc = vf(0, 1024, pat="p (c t) -> p c t", c=4)
        ysq = [vf(1024, 1280), vf(1280, 1536)]
        sig = [vf(1536, 1792), vf(1792, 2048)]
        mean = vf(2048, 2304)
        rstdb = vf(2304, 2560)
        var = vf(2560, 2816)
        kf = vf(2816, 2944, pat="p (h d) -> p h d", h=2)
        ropeA = vf(2944, 3200, pat="p (h d) -> p h d", h=8)
        ropeB = vf(3200, 3456, pat="p (h d) -> p h d", h=8)
        tmpc = [vf(3456, 3712)]
        gfin_v = vf(1024, 2048)
        dg = hTflat[:, 0:124 * 128].rearrange("p (i m) -> p i m", m=128)
        S0, S1, SX = 0, 12288, 24576
        wg_v = [warena[:, s:s + 4096].rearrange("p (c f) -> p c f", c=8) for s in (S0, S1)]
        wu_v = [warena[:, s + 4096:s + 8192].rearrange("p (c f) -> p c f", c=8) for s in (S0, S1)]
        wd_v = [warena[:, s + 8192:s + 12288].rearrange("p (c d) -> p c d", c=4) for s in (S0, S1)]
        win_q = warena[:, S0:S0 + 4096].rearrange("p (c f) -> p c f", c=8)
        win_u = warena[:, S0 + 4096:S0 + 12288].rearrange("p (c f) -> p c f", c=8)
        win_kv = warena[:, SX:SX + 2048].rearrange("p (c f) -> p c f", c=8)
        wout_v = warena[:, S1:S1 + 8192].rearrange("p (c f) -> p c f", c=8)
        hTg = [warena[:, S1 + 8192 + i * 2048:S1 + 8192 + (i + 1) * 2048].rearrange("p (c t) -> p c t", c=8) for i in range(2)]

        def barrier():
            P.add("vector", lambda e: e.memset(junk[:, 0:1], 0.0), writes=[AR, "junk"])

        P.add("sync", lambda e: e.dma_start(out=ident[:], in_=ident_d), writes=["ident"], dma=True)
        P.add("sync", lambda e: e.dma_start(out=posi[:], in_=pos_d), writes=["posi"], dma=True)
        P.add("sync", lambda e: e.dma_start(out=invf[:], in_=invf_d), writes=["invf"], dma=True)
        P.add("sync", lambda e: e.dma_start(out=gains[:], in_=gains_d), writes=["gains"], dma=True)
        for b in range(NB):
            P.add("sync", lambda e, b=b: e.dma_start(out=xs[:, b, :], in_=x_d[b * 128:(b + 1) * 128, :]),
                  writes=[("x", b)], dma=True)
        P.add("sync", lambda e: e.dma_start(out=maskS[:], in_=mask_d), writes=["mask"], dma=True)
        P.add("sync", lambda e: e.dma_start(out=hv[:], in_=hv_d), writes=["hv"], dma=True)
        P.add("sync", lambda e: e.dma_start(out=cvec[:], in_=cvec_d), writes=["cvec"], dma=True)
        P.add("vector", lambda e: e.memset(onesf[:], 1.0), writes=["onesf"])
        P.add("vector", lambda e: e.memset(vx[:], 1.0), writes=[("vx", 0), ("vx", 1), ("vx", 2)])

        P.add("vector", lambda e: e.tensor_copy(out=posf[:], in_=posi[:]), reads=["posi"], writes=["posf"])
        ub = arf[:, 0:NB * 32].rearrange("p (b f) -> p b f", f=32)
        ui = arf[:, 1024:1024 + NB * 32].bitcast(I32).rearrange("p (b f) -> p b f", f=32)
        uf = arf[:, 2048:2048 + NB * 32].rearrange("p (b f) -> p b f", f=32)
        P.add("vector", lambda e: e.tensor_tensor(out=ub, in0=posf[:].unsqueeze(2).to_broadcast([128, NB, 32]),
                                                  in1=invf[:].unsqueeze(1).to_broadcast([128, NB, 32]), op=ALU.mult),
              reads=["posf", "invf", AR], writes=["ub"])
        for name, tab, off in (("sin", sinT, 0.0), ("cos", cosT, 0.25)):
            if off != 0.0:
                P.add("vector", lambda e, off=off: e.tensor_scalar(ub, ub, off, None, ALU.add), reads=["ub", AR], writes=["ub"])
            P.add("vector", lambda e: e.tensor_copy(out=ui, in_=ub), reads=["ub", AR], writes=["ui"])
            P.add("vector", lambda e: e.tensor_copy(out=uf, in_=ui), reads=["ui", AR], writes=["uf"])
            P.add("vector", lambda e: e.tensor_tensor(out=uf, in0=ub, in1=uf, op=ALU.subtract), reads=["ub", "uf", AR], writes=["uf"])
            P.add("scalar", lambda e, tab=tab: e.activation(out=tab[:], in_=uf, func=AF.Sin, scale=2.0 * np.pi),
                  reads=["uf", AR], writes=[name])
        barrier()

        sqj = vb(4096, 5120)

        def norm_blocks(gi, blocks, dest=None):
            b0, b1 = blocks[0], blocks[-1] + 1
            assert list(blocks) == list(range(b0, b1))
            for b in blocks:
                jk, jt = (sqj, "sqj") if dest is None else (hn[b % 2], ("hn", b % 2))
                P.add("scalar", lambda e, b=b, jk=jk: e.activation(out=jk, in_=xs[:, b, :], func=AF.Square, accum_out=ss[:, b:b + 1]),
                      reads=[("x", b), AR], writes=[jt, ("ss", b)])
            sst = [("ss", b) for b in blocks]
            rst = [("rs", b) for b in blocks]
            P.add("vector", lambda e: e.tensor_scalar(rs[:, b0:b1], ss[:, b0:b1], 1.0 / D, EPS, ALU.mult, ALU.add), reads=sst, writes=rst)
            P.add("scalar", lambda e: e.activation(out=rs[:, b0:b1], in_=rs[:, b0:b1], func=AF.Sqrt), reads=rst, writes=rst)
            P.add("vector", lambda e: e.reciprocal(out=rs[:, b0:b1], in_=rs[:, b0:b1]), reads=rst, writes=rst)
            for i, b in enumerate(blocks):
                p = b % 2
                P.add("scalar", lambda e, b=b, p=p: e.activation(out=hn[p], in_=xs[:, b, :], func=AF.Copy, scale=rs[:, b:b + 1]),
                      reads=[("x", b), ("rs", b), AR], writes=[("hn", p)])
                bank, bt = pb()
                for c in range(8):
                    P.add("tensor", lambda e, c=c, p=p, bank=bank: e.transpose(bank[:, c, :], hn[p][:, c * 128:(c + 1) * 128], ident[:]),
                          reads=[("hn", p), "ident", AR], writes=[bt])
                if dest is None:
                    dap, dtok = hT[:, :, b * 128:(b + 1) * 128], ("hT", b)
                else:
                    dap, dtok = dest(i)
                P.add("vector", lambda e, bank=bank, dap=dap: e.tensor_tensor(
                    out=dap, in0=bank[:],
                    in1=gains[:, gi, :].unsqueeze(2).to_broadcast([128, 8, 128]), op=ALU.mult),
                    reads=[bt, "gains", AR], writes=[dtok])

        ffn_groups_all = [[0, 1]] + [list(range(2 + 4 * i, 2 + 4 * i + 4)) for i in range(NOWN // 4)]
        fgroups = [(0, 4), (4, 4), (8, 2), (10, 4), (14, 4), (18, 4)]
        slotctr = [0]

        def ffn(l, gname, uname, dname, post_norm=None, pre_norm=None, post_hook=None, first_block=0):
            ffn_groups = [[b for b in g if b >= first_block] for g in ffn_groups_all]
            ffn_groups = [g for g in ffn_groups if g]
            for fgi, (c0, gsz) in enumerate(fgroups):
                s = slotctr[0] % 2
                slotctr[0] += 1
                f0, f1 = c0 * 128, (c0 + gsz) * 128
                P.add("gpsimd", lambda e, s=s, f0=f0, f1=f1, gsz=gsz: e.dma_start(
                    out=wg_v[s][:, :, 0:gsz * 128], in_=W[gname][l, :, f0:f1].rearrange("(c p) f -> p c f", p=128)),
                    writes=[("slot", s)], dma=True)
                P.add("gpsimd", lambda e, s=s, f0=f0, f1=f1, gsz=gsz: e.dma_start(
                    out=wu_v[s][:, :, 0:gsz * 128], in_=W[uname][l, :, f0:f1].rearrange("(c p) f -> p c f", p=128)),
                    writes=[("slotu", s)], dma=True)
                P.add("gpsimd", lambda e, s=s, f0=f0, f1=f1, gsz=gsz: e.dma_start(
                    out=wd_v[s][:, 0:gsz, :], in_=W[dname][l, f0:f1, :].rearrange("(c p) d -> p c d", p=128)),
                    reads=([AR] if s == 1 else []), writes=[("slotd", s)], dma=True)

                def gu(ti, s=s, gsz=gsz):
                    blocks = ffn_groups[ti]
                    t0, Tg = blocks[0] * 128, len(blocks) * 128
                    p = ti % 2
                    hreads = [("hT", b) for b in blocks]
                    for j in range(gsz):
                        ba, ta = pf()
                        bu, tu = pf()
                        for c in range(8):
                            P.add("tensor", lambda e, c=c, j=j, ba=ba: e.matmul(
                                ba[:, 0:Tg], lhsT=wg_v[s][:, c, j * 128:(j + 1) * 128], rhs=hT[:, c, t0:t0 + Tg],
                                start=(c == 0), stop=(c == 7)), reads=hreads + [("slot", s), AR], writes=[ta])
                        for c in range(8):
                            P.add("tensor", lambda e, c=c, j=j, bu=bu: e.matmul(
                                bu[:, 0:Tg], lhsT=wu_v[s][:, c, j * 128:(j + 1) * 128], rhs=hT[:, c, t0:t0 + Tg],
                                start=(c == 0), stop=(c == 7)), reads=hreads + [("slotu", s), AR], writes=[tu])
                        sp = j % 2
                        P.add("scalar", lambda e, ba=ba, sp=sp: e.activation(out=silu_t[sp][:, 0:Tg], in_=ba[:, 0:Tg], func=AF.Silu),
                              reads=[ta, AR], writes=[("silu", sp)])
                        P.add("vector", lambda e, bu=bu, sp=sp, j=j, p=p: e.tensor_tensor(
                            out=act[p][:, j, 0:Tg], in0=bu[:, 0:Tg], in1=silu_t[sp][:, 0:Tg], op=ALU.mult),
                            reads=[tu, ("silu", sp), AR], writes=[("act", p, j)])

                def down(ti, s=s, gsz=gsz):
                    blocks = ffn_groups[ti]
                    p = ti % 2
                    for bl, b in enumerate(blocks):
                        for half in range(2):
                            bk, tk = pf()
                            for j in range(gsz):
                                P.add("tensor", lambda e, j=j, bl=bl, half=half, bk=bk: e.matmul(
                                    bk[:, :], lhsT=act[p][:, j, bl * 128:(bl + 1) * 128],
                                    rhs=wd_v[s][:, j, half * 512:(half + 1) * 512],
                                    start=(j == 0), stop=(j == gsz - 1)),
                                    reads=[("act", p, j), ("slotd", s), AR], writes=[tk])
                            P.add("vector", lambda e, b=b, half=half, bk=bk: e.scalar_tensor_tensor(
                                out=xs[:, b, half * 512:(half + 1) * 512], in0=bk[:, :], scalar=0.5,
                                in1=xs[:, b, half * 512:(half + 1) * 512], op0=ALU.mult, op1=ALU.add),
                                reads=[tk, ("x", b)], writes=[("x", b)])

                ng = len(ffn_groups)
                pn = pre_norm is not None and fgi == 0
                if pn:
                    norm_blocks(pre_norm, ffn_groups[0])
                    if ng > 1:
                        norm_blocks(pre_norm, ffn_groups[1])
                gu(0)
                for ti in range(ng):
                    if pn and ti + 2 < ng:
                        norm_blocks(pre_norm, ffn_groups[ti + 2])
                    if ti + 1 < ng:
                        gu(ti + 1)
                    down(ti)
                    if fgi == len(fgroups) - 1:
                        if post_norm is not None:
                            norm_blocks(post_norm, ffn_groups[ti])
                        if post_hook is not None:
                            post_hook(ffn_groups[ti])

        mix_groups = [[2 * i, 2 * i + 1] for i in range(NB // 2)]

        def mixer(l):
            P.add("gpsimd", lambda e: e.dma_start(out=win_q, in_=W["w_in"][l, :, 0:512].rearrange("(c p) f -> p c f", p=128)),
                  writes=[("slot", 0)], dma=True)
            P.add("gpsimd", lambda e: e.dma_start(out=win_kv, in_=W["w_in"][l, :, 512:768].rearrange("(c p) f -> p c f", p=128)),
                  writes=["slotx"], dma=True)
            P.add("gpsimd", lambda e: e.dma_start(out=win_u, in_=W["w_in"][l, :, 768:1792].rearrange("(c p) f -> p c f", p=128)),
                  writes=[("slotu", 0), ("slotd", 0)], dma=True)
            P.add("gpsimd", lambda e: e.dma_start(out=wout_v, in_=W["w_out"][l].rearrange("(c p) f -> p c f", p=128)),
                  writes=[("slot", 1), ("slotu", 1), ("slotd", 1)], dma=True)
            WQ, WU, WKV, WO = [("slot", 0)], [("slotu", 0), ("slotd", 0)], ["slotx"], [("slot", 1), ("slotu", 1), ("slotd", 1)]
            P.add("sync", lambda e: e.dma_start(out=esink[:], in_=sinks_d[l].partition_broadcast(128)), writes=["esink"], dma=True)
            P.add("scalar", lambda e: e.activation(out=esink[:], in_=esink[:], func=AF.Exp), reads=["esink"], writes=["esink"])
            cv = lambda c, k: cvec[:, l, c, k:k + 1]
            dg_built = set()
            P.add("vector", lambda e: e.memset(qT[64:128, :], 0.0), reads=[AR], writes=["qT"])
            for i_ in range(2):
                P.add("vector", lambda e, i_=i_: e.memset(kT[i_][64:128, :, :], 0.0), reads=[AR], writes=[("kT", i_)])
            def group_gen(gi, blocks):
                gp = gi % 2
                hg, hgo = hg2[gp], hg2[1 - gp]
                light = (l == DEPTH - 1 and gi == 0)
                norm_blocks(3 * l + 1, blocks, dest=lambda i, gp=gp: (hTg[gp][:, :, i * 128:(i + 1) * 128], ("hTg", gp, i)))
                yield
                hreads = [("hTg", gp, 0), ("hTg", gp, 1), AR]
                if gi == 0:
                    P.add("vector", lambda e: e.memset(hg[:, :, 0:30], 0.0), reads=[AR], writes=[("hg", gp)])
                else:
                    P.add("vector", lambda e: e.tensor_copy(out=hg[:, :, 0:30], in_=hgo[:, :, 256:286]), reads=[AR, ("hg", 1 - gp)], writes=[("hg", gp)])
                hvc = hv[:, 0:1] if gi == 0 else hv[:, 1:2]
                for c in range(4):
                    bk, tk = pf()
                    for kc in range(8):
                        P.add("tensor", lambda e, c=c, kc=kc, bk=bk, gp=gp: e.matmul(
                            bk[:, 0:256], lhsT=win_u[:, kc, c * 128:(c + 1) * 128], rhs=hTg[gp][:, kc, :],
                            start=(kc == 0), stop=(kc == 7)), reads=hreads + WU, writes=[tk])
                    for kc in range(8):
                        P.add("tensor", lambda e, c=c, kc=kc, bk=bk, gp=gp: e.matmul(
                            bk[:, 256:512], lhsT=win_u[:, kc, 512 + c * 128:512 + (c + 1) * 128], rhs=hTg[gp][:, kc, :],
                            start=(kc == 0), stop=(kc == 7)), reads=hreads + WU, writes=[tk])
                    sp = c % 2
                    P.add("scalar", lambda e, bk=bk, sp=sp: e.activation(out=sig[sp], in_=bk[:, 256:512], func=AF.Sigmoid),
                          reads=[tk, AR], writes=[("sig", sp)])
                    P.add("vector", lambda e, bk=bk, sp=sp, c=c, hvc=hvc: e.scalar_tensor_tensor(
                        out=hg[:, c, 30:286], in0=bk[:, 0:256], scalar=hvc, in1=sig[sp], op0=ALU.mult, op1=ALU.mult),
                        reads=[tk, ("sig", sp), "hv", AR, ("hg", gp)], writes=[("hg", gp), ("hgc", gp, c)])
                yield
                conv_pieces = []
                cbank = {}

                def conv_piece(c):
                    if c not in dg_built:
                        dg_built.add(c)
                        for k in range(CW):
                            P.add("vector", lambda e, k=k: e.tensor_scalar(dg[:, c * CW + k, :], ident[:], cv(c, k), None, ALU.mult),
                                  reads=["ident", "cvec", AR], writes=[("dg", c)])
                    cp, ch = divmod(c, 2)
                    if ch == 0:
                        cbank[cp] = pf()
                    bkc, tkc = cbank[cp]
                    for k in range(CW):
                        P.add("tensor", lambda e, k=k: e.matmul(
                            bkc[:, ch * 256:(ch + 1) * 256], lhsT=dg[:, c * CW + k, :], rhs=hg[:, c, k:k + 256],
                            start=(k == 0), stop=(k == CW - 1)),
                            reads=[("dg", c), ("hgc", gp, c), ("hg", gp), AR], writes=[tkc])
                    if ch == 1:
                        for ch2 in range(2):
                            c2 = 2 * cp + ch2
                            P.add("scalar", lambda e, c2=c2, ch2=ch2: e.activation(
                                out=acc[:, c2, :], in_=bkc[:, ch2 * 256:(ch2 + 1) * 256], func=AF.Identity, bias=cv(c2, 31)),
                                reads=[tkc, "cvec", AR], writes=[("acc", c2)])
                if not light:
                    for c in range(4):
                        conv_pieces.append(lambda c=c: conv_piece(c))
                def block_gen(bl, b):
                    if light and b == 0:
                        return
                    par = b % 2
                    vs, vsp = b % 3, (b - 1) % 3
                    qp, tq = pf()
                    for kc in range(8):
                        P.add("tensor", lambda e, kc=kc: e.matmul(
                            qp[:, :], lhsT=hTg[gp][:, kc, bl * 128:(bl + 1) * 128], rhs=win_q[:, kc, :],
                            start=(kc == 0), stop=(kc == 7)), reads=[("hTg", gp, bl), AR] + WQ, writes=[tq])
                    kvp, tkv = pf()
                    for kc in range(8):
                        P.add("tensor", lambda e, kc=kc: e.matmul(
                            kvp[:, 0:256], lhsT=hTg[gp][:, kc, bl * 128:(bl + 1) * 128], rhs=win_kv[:, kc, :],
                            start=(kc == 0), stop=(kc == 7)), reads=[("hTg", gp, bl), AR] + WKV, writes=[tkv])
                    P.add("scalar", lambda e: e.copy(
                        out=vx[:, vs, :, 0:64], in_=kvp[:, 128:256].rearrange("p (g d) -> p g d", g=2)),
                        reads=[tkv], writes=[("vx", vs)])
                    yield
                    cb = cosT[:, b, :].unsqueeze(1).to_broadcast([128, 8, 32])
                    sbq = sinT[:, b, :].unsqueeze(1).to_broadcast([128, 8, 32])
                    cb2 = cosT[:, b, :].unsqueeze(1).to_broadcast([128, 2, 32])
                    sb2 = sinT[:, b, :].unsqueeze(1).to_broadcast([128, 2, 32])
                    q3 = qp[:, :].rearrange("p (h d) -> p h d", h=8)
                    k3 = kvp[:, 0:128].rearrange("p (h d) -> p h d", h=2)
                    TT = lambda out, a, bb, op, reads, writes: P.add(
                        "vector", lambda e: e.tensor_tensor(out=out, in0=a, in1=bb, op=op), reads=reads + [AR], writes=writes)
                    ropeC = ysq[0].rearrange("p (h d) -> p h d", h=8)
                    ropeD = ysq[1].rearrange("p (h d) -> p h d", h=8)
                    YC, YD = ("ysq", 0), ("ysq", 1)
                    TT(ropeA, q3[:, :, 0:32], cb, ALU.mult, [tq, "cos"], ["ropeA"])
                    TT(ropeB, q3[:, :, 32:64], sbq, ALU.mult, [tq, "sin"], ["ropeB"])
                    TT(ropeC, q3[:, :, 32:64], cb, ALU.mult, [tq, "cos"], [YC])
                    TT(ropeD, q3[:, :, 0:32], sbq, ALU.mult, [tq, "sin"], [YD])
                    TT(q_r[:, :, 0:32], ropeA, ropeB, ALU.subtract, ["ropeA", "ropeB"], [("q_r", 0)])
                    TT(q_r[:, :, 32:64], ropeC, ropeD, ALU.add, [YC, YD], [("q_r", 1)])
                    rA2, rB2, rC2, rD2 = ropeA[:, 0:2, :], ropeB[:, 0:2, :], ropeC[:, 0:2, :], ropeD[:, 0:2, :]
                    TT(rA2, k3[:, :, 0:32], cb2, ALU.mult, [tkv, "cos"], ["ropeA"])
                    TT(rB2, k3[:, :, 32:64], sb2, ALU.mult, [tkv, "sin"], ["ropeB"])
                    TT(rC2, k3[:, :, 32:64], cb2, ALU.mult, [tkv, "cos"], [YC])
                    TT(rD2, k3[:, :, 0:32], sb2, ALU.mult, [tkv, "sin"], [YD])
                    TT(k_r[:, :, 0:32], rA2, rB2, ALU.subtract, ["ropeA", "ropeB"], [("k_r", 0)])
                    TT(k_r[:, :, 32:64], rC2, rD2, ALU.add, [YC, YD], [("k_r", 1)])
                    yield
                    pq, tpq = pb()
                    for h in range(8):
                        P.add("tensor", lambda e, h=h: e.transpose(pq[0:64, h, :], q_r[:, h, :], ident[:]),
                              reads=[("q_r", 0), ("q_r", 1), "ident", AR], writes=[tpq])
                    P.add("scalar", lambda e: e.copy(out=qT[0:64, :], in_=pq[0:64, :, :].rearrange("p h t -> p (h t)")),
                          reads=[tpq, AR], writes=["qT"])
                    pk, tpk = pb()
                    for g in range(2):
                        P.add("tensor", lambda e, g=g: e.transpose(pk[0:64, g, :], k_r[:, g, :], ident[:]),
                              reads=[("k_r", 0), ("k_r", 1), "ident", AR], writes=[tpk])
                    P.add("scalar", lambda e: e.copy(out=kT[par][0:64, :, :], in_=pk[0:64, 0:2, :]),
                          reads=[tpk, AR], writes=[("kT", par)])
                    if light:
                        return
                    yield
                    kbs = ([] if b == 0 else [(0, 1 - par, 2 if b == 2 else 1, vsp)]) + [(1, par, 0, vs)]
                    for g in range(2):
                        for (ki, kpar, mi, vsl) in kbs:
                            stb, tst = pf()
                            P.add("tensor", lambda e, g=g, kpar=kpar, stb=stb: e.matmul(
                                stb[:, :], lhsT=kT[kpar][:, g, :], rhs=qT[:, g * 512:(g + 1) * 512],
                                start=True, stop=False), reads=[("kT", kpar), "qT", AR], writes=[tst])
                            P.add("tensor", lambda e, mi=mi, stb=stb: e.matmul(
                                stb[:, :], lhsT=ident[:], rhs=maskS[:, mi, :], start=False, stop=True),
                                reads=["ident", "mask"], writes=[tst])
                            P.add("scalar", lambda e, g=g, ki=ki, stb=stb: e.activation(
                                out=PT[:, g, ki, :], in_=stb[:, :], func=AF.Exp, scale=0.125),
                                reads=[tst, AR], writes=[("PT", g, ki)])
                    yield
                    ovs = []
                    for g in range(2):
                        ov, tov = pf()
                        ovs.append((ov, tov))
                        for hl in range(4):
                            for n, (ki, kpar, mi, vsl) in enumerate(kbs):
                                P.add("tensor", lambda e, g=g, hl=hl, ki=ki, vsl=vsl, ov=ov, n=n, last=(n == len(kbs) - 1): e.matmul(
                                    ov[:, hl * 65:(hl + 1) * 65], lhsT=PT[:, g, ki, hl * 128:(hl + 1) * 128],
                                    rhs=vx[:, vsl, g, :], start=(n == 0), stop=last),
                                    reads=[("PT", g, ki), ("vx", vsl), AR], writes=[tov])
                        ov3 = ov[:, 0:260].rearrange("p (h d) -> p h d", h=4)
                        P.add("vector", lambda e, g=g, ov3=ov3: e.tensor_tensor(
                            out=den[:, 4 * g:4 * g + 4], in0=ov3[:, :, 64], in1=esink[:, 4 * g:4 * g + 4], op=ALU.add),
                            reads=[tov, "esink"], writes=[("den", g)])
                    P.add("vector", lambda e: e.reciprocal(out=den[:], in_=den[:]), reads=[("den", 0), ("den", 1)],
                          writes=[("den", 0), ("den", 1)])
                    for g in range(2):
                        ov, tov = ovs[g]
                        ov3 = ov[:, 0:260].rearrange("p (h d) -> p h d", h=4)
                        P.add("vector", lambda e, g=g, ov3=ov3: e.tensor_tensor(
                            out=attn_tok[:, 4 * g:4 * g + 4, :], in0=ov3[:, :, 0:64],
                            in1=den[:, 4 * g:4 * g + 4].unsqueeze(2).to_broadcast([128, 4, 64]), op=ALU.mult),
                            reads=[tov, ("den", g), AR], writes=[("atok", g)])
                    yield
                    pa, tpa = pb()
                    for c in range(4):
                        P.add("tensor", lambda e, c=c: e.transpose(pa[:, c, :], attn_tok2[:, c * 128:(c + 1) * 128], ident[:]),
                              reads=[("atok", 0), ("atok", 1), "ident", AR], writes=[tpa])
                    P.add("scalar", lambda e: e.copy(out=attnT[:, :, bl * 128:(bl + 1) * 128], in_=pa[:, 0:4, :]),
                          reads=[tpa, AR], writes=[("attnT", bl)])

                bg = [block_gen(bl, b) for bl, b in enumerate(blocks)]

                def step(i):
                    try:
                        next(bg[i])
                    except StopIteration:
                        pass

                def cpiece():
                    if conv_pieces:
                        conv_pieces.pop(0)()
                step(0)
                yield
                for item in (1, 0, 0, "c", 1, 0, 1, "c", 0, 1, "c", 0, 1, "c", 1, 0, 1):
                    if item == "c":
                        cpiece()
                    else:
                        step(item)
                if light:
                    yield
                    return
                while conv_pieces:
                    conv_pieces.pop(0)()
                b1, t1 = pf()
                b2, t2 = pf()
                for c in range(4):
                    sp = c % 2
                    P.add("scalar", lambda e, c=c, sp=sp: e.activation(out=ysq[sp], in_=acc[:, c, :], func=AF.Square),
                          reads=[("acc", c), AR], writes=[("ysq", sp)])
                    P.add("tensor", lambda e, c=c, b1=b1: e.matmul(b1[:, 0:256], lhsT=onesf[:], rhs=acc[:, c, :], start=(c == 0), stop=(c == 3)),
                          reads=[("acc", c), "onesf", AR], writes=[t1])
                    P.add("tensor", lambda e, c=c, sp=sp, b2=b2: e.matmul(b2[:, 0:256], lhsT=onesf[:], rhs=ysq[sp], start=(c == 0), stop=(c == 3)),
                          reads=[("ysq", sp), "onesf", AR], writes=[t2])
                yield
                P.add("vector", lambda e, b1=b1: e.tensor_scalar(mean, b1[:, 0:256], 1.0 / 512, None, ALU.mult), reads=[t1, AR], writes=["mean"])
                P.add("vector", lambda e: e.tensor_tensor(out=var, in0=mean, in1=mean, op=ALU.mult), reads=["mean", AR], writes=["var"])
                P.add("vector", lambda e, b2=b2: e.scalar_tensor_tensor(out=var, in0=b2[:, 0:256], scalar=1.0 / 512, in1=var,
                                                                 op0=ALU.mult, op1=ALU.subtract), reads=[t2, "var", AR], writes=["var"])
                P.add("vector", lambda e: e.tensor_scalar(var, var, EPS, None, ALU.add), reads=["var", AR], writes=["var"])
                P.add("scalar", lambda e: e.activation(out=var, in_=var, func=AF.Sqrt), reads=["var", AR], writes=["var"])
                P.add("vector", lambda e: e.reciprocal(out=rstdb, in_=var), reads=["var", AR], writes=["rstdb"])
                for c in range(4):
                    tp, tpt = (tmpc[0], "tmpc0") if c % 2 == 0 else (vf(2944, 3200), "ropeA")
                    P.add("vector", lambda e, c=c, tp=tp: e.tensor_tensor(out=tp, in0=acc[:, c, :], in1=mean, op=ALU.subtract),
                          reads=[("acc", c), "mean", AR], writes=[tpt])
                    P.add("vector", lambda e, tp=tp: e.tensor_tensor(out=tp, in0=tp, in1=rstdb, op=ALU.mult),
                          reads=[tpt, "rstdb", AR], writes=[tpt])
                    P.add("scalar", lambda e, c=c, tp=tp: e.activation(out=convo[:, c, :], in_=tp, func=AF.Silu,
                                                                        scale=cv(c, 32), bias=cv(c, 33)),
                          reads=[tpt, "cvec", AR], writes=[("convo", c)])
                xos = {}
                for bl, b in enumerate(blocks):
                    for half in range(2):
                        xo, txo = pf()
                        xos[(bl, half)] = (xo, txo)
                        for c in range(4):
                            P.add("tensor", lambda e, c=c, bl=bl, half=half, xo=xo: e.matmul(
                                xo[:, :], lhsT=attnT[:, c, bl * 128:(bl + 1) * 128], rhs=wout_v[:, c, half * 512:(half + 1) * 512],
                                start=(c == 0), stop=False), reads=[("attnT", bl), AR] + WO, writes=[txo])
                yield
                for bl, b in enumerate(blocks):
                    for half in range(2):
                        xo, txo = xos[(bl, half)]
                        for c in range(4):
                            P.add("tensor", lambda e, c=c, bl=bl, half=half, xo=xo: e.matmul(
                                xo[:, :], lhsT=convo[:, c, bl * 128:(bl + 1) * 128], rhs=wout_v[:, 4 + c, half * 512:(half + 1) * 512],
                                start=False, stop=(c == 3)), reads=[("convo", c), AR] + WO, writes=[txo])
                        P.add("vector", lambda e, b=b, half=half, xo=xo: e.tensor_tensor(
                            out=xs[:, b, half * 512:(half + 1) * 512], in0=xo[:, :], in1=xs[:, b, half * 512:(half + 1) * 512], op=ALU.add),
                            reads=[txo, ("x", b)], writes=[("x", b)])

            gens = [group_gen(gi, blocks) for gi, blocks in enumerate(mix_groups)]

            def adv(i):
                if i < len(gens):
                    try:
                        next(gens[i])
                    except StopIteration:
                        pass
            adv(0)
            adv(0)
            adv(0)
            for gi in range(len(gens)):
                adv(gi + 1)
                adv(gi)
                adv(gi + 1)
                adv(gi)
                adv(gi + 1)
                for _ in gens[gi]:
                    pass

        gfin_loaded = [False]

        def final_out(blocks):
            if not gfin_loaded[0]:
                gfin_loaded[0] = True
                P.add("sync", lambda e: e.dma_start(out=gfin_v, in_=gfin_d.partition_broadcast(128)), reads=[AR], writes=["gfin"], dma=True)
            for b in blocks:
                if b < 2:
                    continue
                i = b - 2
                P.add("scalar", lambda e, b=b: e.activation(out=hn[0], in_=xs[:, b, :], func=AF.Square, accum_out=ss[:, b:b + 1]),
                      reads=[("x", b), AR], writes=[("hn", 0), ("ss", b)])
                P.add("vector", lambda e, b=b: e.tensor_scalar(rs[:, b:b + 1], ss[:, b:b + 1], 1.0 / D, EPS, ALU.mult, ALU.add),
                      reads=[("ss", b)], writes=[("rs", b)])
                P.add("scalar", lambda e, b=b: e.activation(out=rs[:, b:b + 1], in_=rs[:, b:b + 1], func=AF.Sqrt),
                      reads=[("rs", b)], writes=[("rs", b)])
                P.add("vector", lambda e, b=b: e.reciprocal(out=rs[:, b:b + 1], in_=rs[:, b:b + 1]),
                      reads=[("rs", b)], writes=[("rs", b)])
                P.add("vector", lambda e, b=b: e.scalar_tensor_tensor(out=xs[:, b, :], in0=xs[:, b, :], scalar=rs[:, b:b + 1],
                                                                      in1=gfin_v, op0=ALU.mult, op1=ALU.mult),
                      reads=[("x", b), ("rs", b), "gfin", AR], writes=[("x", b)])
                P.add("sync", lambda e, i=i, b=b: e.dma_start(out=out_d[i * 128:(i + 1) * 128, :], in_=xs[:, b, :]),
                      reads=[("x", b)], dma=True)

        allb = list(range(NB))
        stage = [0]

        def done():
            stage[0] += 1
            return stop_after is not None and stage[0] >= stop_after

        finished = False
        import os as _os
        KS = int(_os.environ.get("KSTOP", "0"))
        for l in range(DEPTH):
            if KS == 1:
                finished = True
                break
            if KS == 2:
                finished = True
                break
            ffn(l, "ffn1_w_gate", "ffn1_w_up", "ffn1_w_down", pre_norm=(0 if l == 0 else None), first_block=(0 if l == 0 else 1))
            if done():
                finished = True
                break
            barrier()
            mixer(l)
            barrier()
            if done():
                finished = True
                break
            last = (l + 1 == DEPTH) and stop_after is None
            ffn(l, "ffn2_w_gate", "ffn2_w_up", "ffn2_w_down", post_norm=(3 * (l + 1) if l + 1 < DEPTH else None),
                pre_norm=3 * l + 2, post_hook=(final_out if last else None), first_block=(2 if l + 1 == DEPTH else 1))
            if done():
                finished = True
                break
        barrier()
        if finished:
            for i in range(NOWN):
                b = i + 2
                P.add("sync", lambda e, i=i, b=b: e.dma_start(out=out_d[i * 128:(i + 1) * 128, :], in_=xs[:, b, :]),
                      reads=[("x", b)], dma=True)
        else:
            pass
        P.emit(nc)
    return nc


def make_inputs(inputs, n_cores, cores_per_seq, NOWN, DEPTH):
    x = np.asarray(inputs["x"], np.float32)
    pos = np.asarray(inputs["positions"], np.int32)
    B, S, _ = x.shape
    own = NOWN * 128
    NB = NOWN + 2
    bf = ml_dtypes.bfloat16
    i = np.arange(128)[None, :]
    j = np.arange(128)[:, None]
    m_cur = np.where(i >= j, 0.0, NEG).astype(np.float32)
    m_prev = np.where(j > i, 0.0, NEG).astype(np.float32)
    m_none = np.full((128, 128), NEG, np.float32)
    inv_freq = (1.0 / (10000.0 ** (np.arange(0, 64, 2, dtype=np.float32) / 64))).astype(np.float32)
    invf = np.broadcast_to((inv_freq.astype(np.float64) / (2 * np.pi)).astype(np.float32)[None, :], (128, 32)).copy()
    gains = np.zeros((128, 3 * DEPTH, 8), np.float32)
    for l in range(DEPTH):
        for k, n in enumerate(["ffn1_norm", "mix_norm", "ffn2_norm"]):
            gains[:, 3 * l + k, :] = np.asarray(inputs[n], np.float32)[l].reshape(8, 128).T
    cvec = np.zeros((128, DEPTH, 4, 34), np.float32)
    for l in range(DEPTH):
        cw = np.asarray(inputs["conv_w"], np.float32)[l]
        cvec[:, l, :, 0:31] = cw.T.reshape(4, 128, 31).transpose(1, 0, 2)
        for k, n in ((31, "conv_b"), (32, "conv_ln_g"), (33, "conv_ln_b")):
            cvec[:, l, :, k] = np.asarray(inputs[n], np.float32)[l].reshape(4, 128).T
    shared = {
        "ident": np.eye(128).astype(bf), "invf": invf, "gains": gains, "cvec": cvec,
        "gfin": np.asarray(inputs["final_norm"], np.float32),
        "sinks": np.asarray(inputs["attn_sinks"], np.float32),
    }
    for n in WNAMES:
        shared[n] = np.ascontiguousarray(np.asarray(inputs[n], np.float32))
    in_maps = []
    for core in range(n_cores):
        bi, ci = divmod(core, cores_per_seq)
        t0 = ci * own
        xc = np.zeros((NB * 128, D), np.float32)
        pc = np.zeros((NB * 128,), np.int32)
        lo = t0 - 256
        if lo >= 0:
            xc[:] = x[bi, lo:t0 + own]
            pc[:] = pos[bi, lo:t0 + own]
            start = False
        else:
            xc[256:] = x[bi, t0:t0 + own]
            pc[256:] = pos[bi, t0:t0 + own]
            start = True
        mask = np.stack([m_cur, m_prev, m_none if start else m_prev], axis=1)
        mask = np.repeat(mask[:, :, None, :], 4, axis=2).reshape(128, 3, 512).astype(bf)
        hvv = np.ones((128, 2), np.float32)
        hvv[:, 0] = 0.0 if start else 1.0
        m = dict(shared)
        m.update({"x": xc, "pos": np.ascontiguousarray(pc.reshape(NB, 128).T), "mask": mask, "hv": hvv})
        in_maps.append(m)
    return in_maps


def run(inputs, n_cores, cores_per_seq, NOWN, DEPTH=2, stop_after=None, trace=False):
    nc = build(NOWN, DEPTH, stop_after)
    in_maps = make_inputs(inputs, n_cores, cores_per_seq, NOWN, DEPTH)
    res = run_bass_kernel_spmd(nc, in_maps, core_ids=list(range(n_cores)), trace=trace)
    x = np.asarray(inputs["x"])
    B, S, _ = x.shape
    own = NOWN * 128
    out = np.zeros((B, S, D), np.float32)
    for core in range(n_cores):
        bi, ci = divmod(core, cores_per_seq)
        out[bi, ci * own:(ci + 1) * own] = res.results[core]["out"]
    return out, res


def kernel(**inputs):
    out, _ = run(inputs, 8, 4, 16, 2)
    return out
```

```python
import contextlib
import numpy as np
import ml_dtypes
import concourse.bass as bass
import concourse.mybir as mybir
from concourse.bass_utils import run_bass_kernel_spmd

F32 = mybir.dt.float32
BF16 = mybir.dt.bfloat16
I32 = mybir.dt.int32
AF = mybir.ActivationFunctionType
ALU = mybir.AluOpType

ENGINES = ("tensor", "vector", "scalar", "gpsimd", "sync")
NDMA_SEMS = 6
import os as _os0
NDMA_Q = {"gpsimd": 1, "sync": int(_os0.environ.get("NDSY", "2"))}

D = 1024
DFF = 2816
NFC = DFF // 128
DIN = 1792
CW = 31
EPS = 1e-5
NEG = -30000.0


class Op:
    __slots__ = ("eng", "fn", "waits", "signal", "dma", "idx")

    def __init__(self, eng, fn, dma=None):
        self.eng = eng
        self.fn = fn
        self.waits = {}
        self.signal = False
        self.dma = dma
        self.idx = None


class Prog:
    def __init__(self, same_engine_sync=("vector", "scalar", "gpsimd")):
        self.ops = {e: [] for e in ENGINES}
        self.last_w = {}
        self.readers = {}
        self.ndma = {e: 0 for e in ENGINES}
        self.same = set(same_engine_sync)

    def _add_dep(self, op, key, val):
        if key[0] == "e" and key[1] == op.eng and key[1] not in self.same:
            return
        if op.waits.get(key, -1) < val:
            op.waits[key] = val

    def add(self, eng, fn, reads=(), writes=(), dma=False):
        if dma:
            n = self.ndma[eng]
            self.ndma[eng] += 1
            nq = NDMA_Q.get(eng, NDMA_SEMS)
            k = n % nq
            val = 16 * (n // nq + 1)
            op = Op(eng, fn, dma=(eng, k, val))
            if val > 16:
                op.waits[("d", eng, k)] = val - 16
            mykey, myval = ("d", eng, k), val
        else:
            op = Op(eng, fn)
            mykey, myval = ("e", eng), len(self.ops[eng])
        op.idx = len(self.ops[eng])
        for t in reads:
            w = self.last_w.get(t)
            if w is not None:
                self._add_dep(op, w[0], w[1])
        for t in writes:
            w = self.last_w.get(t)
            if w is not None:
                self._add_dep(op, w[0], w[1])
            for k2, v2 in self.readers.get(t, {}).items():
                self._add_dep(op, k2, v2)
        for t in reads:
            r = self.readers.setdefault(t, {})
            if r.get(mykey, -1) < myval:
                r[mykey] = myval
        for t in writes:
            self.last_w[t] = (mykey, myval)
            self.readers[t] = {}
        self.ops[eng].append(op)
        return op

    def emit(self, nc):
        for e in ENGINES:
            for op in self.ops[e]:
                for key, v in op.waits.items():
                    if key[0] == "e":
                        self.ops[key[1]][v].signal = True
        counts = {}
        for e in ENGINES:
            c = 0
            arr = []
            for op in self.ops[e]:
                if op.signal:
                    c += 1
                arr.append(c)
            counts[e] = arr
        with contextlib.ExitStack() as st:
            esem = {e: st.enter_context(nc.semaphore("s_" + e)) for e in ENGINES}
            dsem = {}
            for e in ENGINES:
                if self.ndma[e]:
                    for k in range(NDMA_SEMS):
                        dsem[(e, k)] = st.enter_context(nc.semaphore("d_%s%d" % (e, k)))
            block = st.enter_context(nc.Block())

            def make(e):
                def body(eng):
                    waited = {}
                    for op in self.ops[e]:
                        for key, v in op.waits.items():
                            if key[0] == "e":
                                sem = esem[key[1]]
                                val = counts[key[1]][v]
                            else:
                                sem = dsem[(key[1], key[2])]
                                val = v
                            if waited.get(sem.name, -1) >= val:
                                continue
                            waited[sem.name] = val
                            eng.wait_ge(sem, val)
                        ins = op.fn(eng)
                        if op.dma is not None:
                            ins.then_inc(dsem[(op.dma[0], op.dma[1])], 16)
                        elif op.signal:
                            ins.then_inc(esem[e], 1)
                    if self.ndma[e]:
                        n = self.ndma[e]
                        nq = NDMA_Q.get(e, NDMA_SEMS)
                        for k in range(nq):
                            cnt = (n - k + nq - 1) // nq
                            if cnt > 0:
                                eng.wait_ge(dsem[(e, k)], 16 * cnt)
                return body

            for e in ENGINES:
                if self.ops[e]:
                    getattr(block, e)(make(e))


WNAMES = ["ffn1_w_gate", "ffn1_w_up", "ffn1_w_down", "w_in", "w_out",
          "ffn2_w_gate", "ffn2_w_up", "ffn2_w_down"]
WSHAPES = {"ffn1_w_gate": [D, DFF], "ffn1_w_up": [D, DFF], "ffn1_w_down": [DFF, D],
           "w_in": [D, DIN], "w_out": [D, D],
           "ffn2_w_gate": [D, DFF], "ffn2_w_up": [D, DFF], "ffn2_w_down": [DFF, D]}


def build(NOWN, DEPTH=2, stop_after=None):
    NB = NOWN + 2
    T = NB * 128
    nc = bass.Bass("TRN2", target_bir_lowering=False)
    dr = lambda n, s, d, k="ExternalInput": nc.dram_tensor(n, s, d, kind=k).ap()
    x_d = dr("x", [T, D], F32)
    pos_d = dr("pos", [128, NB], I32)
    mask_d = dr("mask", [128, 3, 512], BF16)
    hv_d = dr("hv", [128, 2], F32)
    ident_d = dr("ident", [128, 128], BF16)
    invf_d = dr("invf", [128, 32], F32)
    gains_d = dr("gains", [128, 3 * DEPTH, 8], F32)
    gfin_d = dr("gfin", [D], F32)
    cvec_d = dr("cvec", [128, DEPTH, 4, 34], F32)
    sinks_d = dr("sinks", [DEPTH, 8], F32)
    W = {n: dr(n, [DEPTH] + WSHAPES[n], F32) for n in WNAMES}
    out_d = dr("out", [NOWN * 128, D], F32, "ExternalOutput")

    P = Prog()
    st = contextlib.ExitStack()
    with st:
        sb = lambda n, s, d: st.enter_context(nc.sbuf_tensor(n, s, d))
        xs = sb("xs", [128, NB, D], F32)
        hTflat = sb("hT", [128, max(8 * T, 124 * 128)], BF16)
        hT = hTflat[:, 0:8 * T].rearrange("p (c t) -> p c t", c=8)
        warena = sb("warena", [128, 26624], BF16)
        arb = sb("arb", [128, 9088], BF16)
        hnt = sb("hnt", [128, 2, 1024], BF16)
        arf = sb("arf", [128, 3712], F32)
        cosT = sb("cosT", [128, NB, 32], F32)
        sinT = sb("sinT", [128, NB, 32], F32)
        maskS = sb("maskS", [128, 3, 512], BF16)
        ident = sb("ident_s", [128, 128], BF16)
        onesf = sb("onesf", [128, 128], F32)
        cvec = sb("cvec_s", [128, DEPTH, 4, 34], F32)
        gains = sb("gains_s", [128, 3 * DEPTH, 8], F32)
        hv = sb("hv_s", [128, 2], F32)
        invf = sb("invf_s", [128, 32], F32)
        posi = sb("posi", [128, NB], I32)
        posf = sb("posf", [128, NB], F32)
        ss = sb("ss", [128, NB], F32)
        rs = sb("rs", [128, NB], F32)
        esink = sb("esink", [128, 8], F32)
        den = sb("den", [128, 8], F32)
        vx = sb("vx", [128, 3, 2, 65], BF16)
        junk = sb("junk", [128, 2], F32)
        pfb = [st.enter_context(nc.psum_tensor("pf%d" % i, [128, 512], F32)) for i in range(6)]
        pbb = [st.enter_context(nc.psum_tensor("pb%d" % i, [128, 8, 128], BF16)) for i in range(2)]
        cnt = {"pf": 0, "pb": 0}

        def pf():
            i = cnt["pf"] % 6
            cnt["pf"] += 1
            return pfb[i], ("pf", i)

        def pb():
            i = cnt["pb"] % 2
            cnt["pb"] += 1
            return pbb[i], ("pb", i)

        AR = "ARENA"

        def vb(a, b, **kw):
            v = arb[:, a:b]
            if kw:
                pat = kw.pop("pat")
                v = v.rearrange(pat, **kw)
            return v

        def vf(a, b, **kw):
            v = arf[:, a:b]
            if kw:
                pat = kw.pop("pat")
                v = v.rearrange(pat, **kw)
            return v

        hn = [hnt[:, 0, :], hnt[:, 1, :]]
        act = [vb(0, 2048, pat="p (c t) -> p c t", c=4), vb(2048, 4096, pat="p (c t) -> p c t", c=4)]
        silu_t = [vf(0, 512), vf(512, 1024)]
        PT = vb(0, 2048, pat="p (g k t) -> p g k t", g=2, k=2)
        attnT = vb(2048, 3072, pat="p (c t) -> p c t", c=4)
        convo = vb(3072, 4096, pat="p (c t) -> p c t", c=4)
        q_r = vb(4096, 4608, pat="p (h d) -> p h d", h=8)
        q_r2 = vb(4096, 4608)
        k_r = vb(4608, 4736, pat="p (h d) -> p h d", h=2)
        attn_tok = vb(4736, 5248, pat="p (h d) -> p h d", h=8)
        attn_tok2 = vb(4736, 5248)
        qT = vb(5248, 6272)
        kT = [vb(6272, 6528, pat="p (g t) -> p g t", g=2), vb(6528, 6784, pat="p (g t) -> p g t", g=2)]
        hg2 = [vb(6784, 7928, pat="p (c t) -> p c t", c=4), vb(7928, 9072, pat="p (c t) -> p c t", c=4)]
        acc = vf(0, 1024, pat="p (c t) -> p c t", c=4)
        ysq = [vf(1024, 1280), vf(1280, 1536)]
        sig = [vf(1536, 1792), vf(1792, 2048)]
        mean = vf(2048, 2304)
        rstdb = vf(2304, 2560)
        var = vf(2560, 2816)
        kf = vf(2816, 2944, pat="p (h d) -> p h d", h=2)
        ropeA = vf(2944, 3200, pat="p (h d) -> p h d", h=8)
        ropeB = vf(3200, 3456, pat="p (h d) -> p h d", h=8)
        tmpc = [vf(3456, 3712)]
        gfin_v = vf(1024, 2048)
        dg = hTflat[:, 0:124 * 128].rearrange("p (i m) -> p i m", m=128)
        S0, S1, SX = 0, 12288, 24576
        wg_v = [warena[:, s:s + 4096].rearrange("p (c f) -> p c f", c=8) for s in (S0, S1)]
        wu_v = [warena[:, s + 4096:s + 8192].rearrange("p (c f) -> p c f", c=8) for s in (S0, S1)]
        wd_v = [warena[:, s + 8192:s + 12288].rearrange("p (c d) -> p c d", c=4) for s in (S0, S1)]
        win_q = warena[:, S0:S0 + 4096].rearrange("p (c f) -> p c f", c=8)
        win_u = warena[:, S0 + 4096:S0 + 12288].rearrange("p (c f) -> p c f", c=8)
        win_kv = warena[:, SX:SX + 2048].rearrange("p (c f) -> p c f", c=8)
        wout_v = warena[:, S1:S1 + 8192].rearrange("p (c f) -> p c f", c=8)
        hTg = [warena[:, S1 + 8192 + i * 2048:S1 + 8192 + (i + 1) * 2048].rearrange("p (c t) -> p c t", c=8) for i in range(2)]

        def barrier():
            P.add("vector", lambda e: e.memset(junk[:, 0:1], 0.0), writes=[AR, "junk"])

        P.add("sync", lambda e: e.dma_start(out=ident[:], in_=ident_d), writes=["ident"], dma=True)
        P.add("sync", lambda e: e.dma_start(out=posi[:], in_=pos_d), writes=["posi"], dma=True)
        P.add("sync", lambda e: e.dma_start(out=invf[:], in_=invf_d), writes=["invf"], dma=True)
        P.add("sync", lambda e: e.dma_start(out=gains[:], in_=gains_d), writes=["gains"], dma=True)
        for b in range(NB):
            P.add("sync", lambda e, b=b: e.dma_start(out=xs[:, b, :], in_=x_d[b * 128:(b + 1) * 128, :]),
                  writes=[("x", b)], dma=True)
        P.add("sync", lambda e: e.dma_start(out=maskS[:], in_=mask_d), writes=["mask"], dma=True)
        P.add("sync", lambda e: e.dma_start(out=hv[:], in_=hv_d), writes=["hv"], dma=True)
        P.add("sync", lambda e: e.dma_start(out=cvec[:], in_=cvec_d), writes=["cvec"], dma=True)
        P.add("vector", lambda e: e.memset(onesf[:], 1.0), writes=["onesf"])
        P.add("vector", lambda e: e.memset(junk[:, 1:2], EPS), writes=["epsT"])
        P.add("vector", lambda e: e.memset(vx[:], 1.0), writes=[("vx", 0), ("vx", 1), ("vx", 2)])

        P.add("vector", lambda e: e.tensor_copy(out=posf[:], in_=posi[:]), reads=["posi"], writes=["posf"])
        ub = arf[:, 0:NB * 32].rearrange("p (b f) -> p b f", f=32)
        ui = arf[:, 1024:1024 + NB * 32].bitcast(I32).rearrange("p (b f) -> p b f", f=32)
        uf = arf[:, 2048:2048 + NB * 32].rearrange("p (b f) -> p b f", f=32)
        P.add("vector", lambda e: e.tensor_tensor(out=ub, in0=posf[:].unsqueeze(2).to_broadcast([128, NB, 32]),
                                                  in1=invf[:].unsqueeze(1).to_broadcast([128, NB, 32]), op=ALU.mult),
              reads=["posf", "invf", AR], writes=["ub"])
        for name, tab, off in (("sin", sinT, 0.0), ("cos", cosT, 0.25)):
            if off != 0.0:
                P.add("vector", lambda e, off=off: e.tensor_scalar(ub, ub, off, None, ALU.add), reads=["ub", AR], writes=["ub"])
            P.add("vector", lambda e: e.tensor_copy(out=ui, in_=ub), reads=["ub", AR], writes=["ui"])
            P.add("vector", lambda e: e.tensor_copy(out=uf, in_=ui), reads=["ui", AR], writes=["uf"])
            P.add("vector", lambda e: e.tensor_tensor(out=uf, in0=ub, in1=uf, op=ALU.subtract), reads=["ub", "uf", AR], writes=["uf"])
            P.add("scalar", lambda e, tab=tab: e.activation(out=tab[:], in_=uf, func=AF.Sin, scale=2.0 * np.pi),
                  reads=["uf", AR], writes=[name])
        barrier()

        sqj = vb(4096, 5120)

        def norm_blocks(gi, blocks, dest=None):
            b0, b1 = blocks[0], blocks[-1] + 1
            assert list(blocks) == list(range(b0, b1))
            for b in blocks:
                jk, jt = (sqj, "sqj") if dest is None else (hn[b % 2], ("hn", b % 2))
                P.add("scalar", lambda e, b=b, jk=jk: e.activation(out=jk, in_=xs[:, b, :], func=AF.Square, accum_out=ss[:, b:b + 1]),
                      reads=[("x", b), AR], writes=[jt, ("ss", b)])
            sst = [("ss", b) for b in blocks]
            rst = [("rs", b) for b in blocks]
            P.add("vector", lambda e: e.tensor_scalar(rs[:, b0:b1], ss[:, b0:b1], 1.0 / D, EPS, ALU.mult, ALU.add), reads=sst, writes=rst)
            P.add("scalar", lambda e: e.activation(out=rs[:, b0:b1], in_=rs[:, b0:b1], func=AF.Sqrt), reads=rst, writes=rst)
            P.add("vector", lambda e: e.reciprocal(out=rs[:, b0:b1], in_=rs[:, b0:b1]), reads=rst, writes=rst)
            for i, b in enumerate(blocks):
                p = b % 2
                P.add("scalar", lambda e, b=b, p=p: e.activation(out=hn[p], in_=xs[:, b, :], func=AF.Copy, scale=rs[:, b:b + 1]),
                      reads=[("x", b), ("rs", b), AR], writes=[("hn", p)])
                bank, bt = pb()
                for c in range(8):
                    P.add("tensor", lambda e, c=c, p=p, bank=bank: e.transpose(bank[:, c, :], hn[p][:, c * 128:(c + 1) * 128], ident[:]),
                          reads=[("hn", p), "ident", AR], writes=[bt])
                if dest is None:
                    dap, dtok = hT[:, :, b * 128:(b + 1) * 128], ("hT", b)
                else:
                    dap, dtok = dest(i)
                P.add("vector", lambda e, bank=bank, dap=dap: e.tensor_tensor(
                    out=dap, in0=bank[:],
                    in1=gains[:, gi, :].unsqueeze(2).to_broadcast([128, 8, 128]), op=ALU.mult),
                    reads=[bt, "gains", AR], writes=[dtok])

        ffn_groups_all = [[0, 1]] + [list(range(2 + 4 * i, 2 + 4 * i + 4)) for i in range(NOWN // 4)]
        fgroups = [(0, 4), (4, 4), (8, 2), (10, 4), (14, 4), (18, 4)]
        slotctr = [0]

        def ffn(l, gname, uname, dname, post_norm=None, pre_norm=None, post_hook=None, first_block=0):
            ffn_groups = [[b for b in g if b >= first_block] for g in ffn_groups_all]
            ffn_groups = [g for g in ffn_groups if g]
            for fgi, (c0, gsz) in enumerate(fgroups):
                s = slotctr[0] % 2
                slotctr[0] += 1
                f0, f1 = c0 * 128, (c0 + gsz) * 128
                P.add("gpsimd", lambda e, s=s, f0=f0, f1=f1, gsz=gsz: e.dma_start(
                    out=wg_v[s][:, :, 0:gsz * 128], in_=W[gname][l, :, f0:f1].rearrange("(c p) f -> p c f", p=128)),
                    writes=[("slot", s)], dma=True)
                P.add("gpsimd", lambda e, s=s, f0=f0, f1=f1, gsz=gsz: e.dma_start(
                    out=wu_v[s][:, :, 0:gsz * 128], in_=W[uname][l, :, f0:f1].rearrange("(c p) f -> p c f", p=128)),
                    writes=[("slotu", s)], dma=True)
                P.add("gpsimd", lambda e, s=s, f0=f0, f1=f1, gsz=gsz: e.dma_start(
                    out=wd_v[s][:, 0:gsz, :], in_=W[dname][l, f0:f1, :].rearrange("(c p) d -> p c d", p=128)),
                    reads=([AR] if s == 1 else []), writes=[("slotd", s)], dma=True)

                def gu(ti, s=s, gsz=gsz):
                    blocks = ffn_groups[ti]
                    t0, Tg = blocks[0] * 128, len(blocks) * 128
                    p = ti % 2
                    hreads = [("hT", b) for b in blocks]
                    for j in range(gsz):
                        ba, ta = pf()
                        bu, tu = pf()
                        for c in range(8):
                            P.add("tensor", lambda e, c=c, j=j, ba=ba: e.matmul(
                                ba[:, 0:Tg], lhsT=wg_v[s][:, c, j * 128:(j + 1) * 128], rhs=hT[:, c, t0:t0 + Tg],
                                start=(c == 0), stop=(c == 7)), reads=hreads + [("slot", s), AR], writes=[ta])
                        for c in range(8):
                            P.add("tensor", lambda e, c=c, j=j, bu=bu: e.matmul(
                                bu[:, 0:Tg], lhsT=wu_v[s][:, c, j * 128:(j + 1) * 128], rhs=hT[:, c, t0:t0 + Tg],
                                start=(c == 0), stop=(c == 7)), reads=hreads + [("slotu", s), AR], writes=[tu])
                        sp = j % 2
                        P.add("scalar", lambda e, ba=ba, sp=sp: e.activation(out=silu_t[sp][:, 0:Tg], in_=ba[:, 0:Tg], func=AF.Silu),
                              reads=[ta, AR], writes=[("silu", sp)])
                        P.add("vector", lambda e, bu=bu, sp=sp, j=j, p=p: e.tensor_tensor(
                            out=act[p][:, j, 0:Tg], in0=bu[:, 0:Tg], in1=silu_t[sp][:, 0:Tg], op=ALU.mult),
                            reads=[tu, ("silu", sp), AR], writes=[("act", p, j)])

                def down(ti, s=s, gsz=gsz):
                    blocks = ffn_groups[ti]
                    p = ti % 2
                    for bl, b in enumerate(blocks):
                        for half in range(2):
                            bk, tk = pf()
                            for j in range(gsz):
                                P.add("tensor", lambda e, j=j, bl=bl, half=half, bk=bk: e.matmul(
                                    bk[:, :], lhsT=act[p][:, j, bl * 128:(bl + 1) * 128],
                                    rhs=wd_v[s][:, j, half * 512:(half + 1) * 512],
                                    start=(j == 0), stop=(j == gsz - 1)),
                                    reads=[("act", p, j), ("slotd", s), AR], writes=[tk])
                            P.add("vector", lambda e, b=b, half=half, bk=bk: e.scalar_tensor_tensor(
                                out=xs[:, b, half * 512:(half + 1) * 512], in0=bk[:, :], scalar=0.5,
                                in1=xs[:, b, half * 512:(half + 1) * 512], op0=ALU.mult, op1=ALU.add),
                                reads=[tk, ("x", b)], writes=[("x", b)])

                ng = len(ffn_groups)
                pn = pre_norm is not None and fgi == 0
                if pn:
                    norm_blocks(pre_norm, ffn_groups[0])
                    if ng > 1:
                        norm_blocks(pre_norm, ffn_groups[1])
                gu(0)
                for ti in range(ng):
                    if pn and ti + 2 < ng:
                        norm_blocks(pre_norm, ffn_groups[ti + 2])
                    if ti + 1 < ng:
                        gu(ti + 1)
                    down(ti)
                    if fgi == len(fgroups) - 1:
                        if post_norm is not None:
                            norm_blocks(post_norm, ffn_groups[ti])
                        if post_hook is not None:
                            post_hook(ffn_groups[ti])

        mix_groups = [[2 * i, 2 * i + 1] for i in range(NB // 2)]

        def mixer(l):
            P.add("gpsimd", lambda e: e.dma_start(out=win_q, in_=W["w_in"][l, :, 0:512].rearrange("(c p) f -> p c f", p=128)),
                  writes=[("slot", 0)], dma=True)
            P.add("gpsimd", lambda e: e.dma_start(out=win_kv, in_=W["w_in"][l, :, 512:768].rearrange("(c p) f -> p c f", p=128)),
                  writes=["slotx"], dma=True)
            P.add("gpsimd", lambda e: e.dma_start(out=win_u, in_=W["w_in"][l, :, 768:1792].rearrange("(c p) f -> p c f", p=128)),
                  writes=[("slotu", 0), ("slotd", 0)], dma=True)
            P.add("gpsimd", lambda e: e.dma_start(out=wout_v, in_=W["w_out"][l].rearrange("(c p) f -> p c f", p=128)),
                  writes=[("slot", 1), ("slotu", 1), ("slotd", 1)], dma=True)
            WQ, WU, WKV, WO = [("slot", 0)], [("slotu", 0), ("slotd", 0)], ["slotx"], [("slot", 1), ("slotu", 1), ("slotd", 1)]
            P.add("sync", lambda e: e.dma_start(out=esink[:], in_=sinks_d[l].partition_broadcast(128)), writes=["esink"], dma=True)
            P.add("scalar", lambda e: e.activation(out=esink[:], in_=esink[:], func=AF.Exp), reads=["esink"], writes=["esink"])
            cv = lambda c, k: cvec[:, l, c, k:k + 1]
            dg_built = set()
            P.add("vector", lambda e: e.memset(qT[64:128, :], 0.0), reads=[AR], writes=["qT"])
            for i_ in range(2):
                P.add("vector", lambda e, i_=i_: e.memset(kT[i_][64:128, :, :], 0.0), reads=[AR], writes=[("kT", i_)])
            def group_gen(gi, blocks):
                gp = gi % 2
                hg, hgo = hg2[gp], hg2[1 - gp]
                light = (l == DEPTH - 1 and gi == 0)
                norm_blocks(3 * l + 1, blocks, dest=lambda i, gp=gp: (hTg[gp][:, :, i * 128:(i + 1) * 128], ("hTg", gp, i)))
                yield
                hreads = [("hTg", gp, 0), ("hTg", gp, 1), AR]
                if gi == 0:
                    P.add("vector", lambda e: e.memset(hg[:, :, 0:30], 0.0), reads=[AR], writes=[("hg", gp)])
                else:
                    P.add("vector", lambda e: e.tensor_copy(out=hg[:, :, 0:30], in_=hgo[:, :, 256:286]), reads=[AR, ("hg", 1 - gp)], writes=[("hg", gp)])
                hvc = hv[:, 0:1] if gi == 0 else hv[:, 1:2]
                for c in range(4):
                    bk, tk = pf()
                    for kc in range(8):
                        P.add("tensor", lambda e, c=c, kc=kc, bk=bk, gp=gp: e.matmul(
                            bk[:, 0:256], lhsT=win_u[:, kc, c * 128:(c + 1) * 128], rhs=hTg[gp][:, kc, :],
                            start=(kc == 0), stop=(kc == 7)), reads=hreads + WU, writes=[tk])
                    for kc in range(8):
                        P.add("tensor", lambda e, c=c, kc=kc, bk=bk, gp=gp: e.matmul(
                            bk[:, 256:512], lhsT=win_u[:, kc, 512 + c * 128:512 + (c + 1) * 128], rhs=hTg[gp][:, kc, :],
                            start=(kc == 0), stop=(kc == 7)), reads=hreads + WU, writes=[tk])
                    sp = c % 2
                    P.add("scalar", lambda e, bk=bk, sp=sp: e.activation(out=sig[sp], in_=bk[:, 256:512], func=AF.Sigmoid),
                          reads=[tk, AR], writes=[("sig", sp)])
                    P.add("vector", lambda e, bk=bk, sp=sp, c=c, hvc=hvc: e.scalar_tensor_tensor(
                        out=hg[:, c, 30:286], in0=bk[:, 0:256], scalar=hvc, in1=sig[sp], op0=ALU.mult, op1=ALU.mult),
                        reads=[tk, ("sig", sp), "hv", AR, ("hg", gp)], writes=[("hg", gp), ("hgc", gp, c)])
                yield
                conv_pieces = []
                cbank = {}

                def conv_piece(c):
                    if c not in dg_built:
                        dg_built.add(c)
                        for k in range(CW):
                            P.add("vector", lambda e, k=k: e.tensor_scalar(dg[:, c * CW + k, :], ident[:], cv(c, k), None, ALU.mult),
                                  reads=["ident", "cvec", AR], writes=[("dg", c)])
                    cp, ch = divmod(c, 2)
                    if ch == 0:
                        cbank[cp] = pf()
                    bkc, tkc = cbank[cp]
                    for k in range(CW):
                        P.add("tensor", lambda e, k=k: e.matmul(
                            bkc[:, ch * 256:(ch + 1) * 256], lhsT=dg[:, c * CW + k, :], rhs=hg[:, c, k:k + 256],
                            start=(k == 0), stop=(k == CW - 1)),
                            reads=[("dg", c), ("hgc", gp, c), ("hg", gp), AR], writes=[tkc])
                    if ch == 1:
                        for ch2 in range(2):
                            c2 = 2 * cp + ch2
                            P.add("scalar", lambda e, c2=c2, ch2=ch2: e.activation(
                                out=acc[:, c2, :], in_=bkc[:, ch2 * 256:(ch2 + 1) * 256], func=AF.Identity, bias=cv(c2, 31)),
                                reads=[tkc, "cvec", AR], writes=[("acc", c2)])
                if not light:
                    for c in range(4):
                        conv_pieces.append(lambda c=c: conv_piece(c))
                def block_gen(bl, b):
                    if light and b == 0:
                        return
                    par = b % 2
                    vs, vsp = b % 3, (b - 1) % 3
                    qp, tq = pf()
                    for kc in range(8):
                        P.add("tensor", lambda e, kc=kc: e.matmul(
                            qp[:, :], lhsT=hTg[gp][:, kc, bl * 128:(bl + 1) * 128], rhs=win_q[:, kc, :],
                            start=(kc == 0), stop=(kc == 7)), reads=[("hTg", gp, bl), AR] + WQ, writes=[tq])
                    kvp, tkv = pf()
                    for kc in range(8):
                        P.add("tensor", lambda e, kc=kc: e.matmul(
                            kvp[:, 0:256], lhsT=hTg[gp][:, kc, bl * 128:(bl + 1) * 128], rhs=win_kv[:, kc, :],
                            start=(kc == 0), stop=(kc == 7)), reads=[("hTg", gp, bl), AR] + WKV, writes=[tkv])
                    P.add("scalar", lambda e: e.copy(
                        out=vx[:, vs, :, 0:64], in_=kvp[:, 128:256].rearrange("p (g d) -> p g d", g=2)),
                        reads=[tkv], writes=[("vx", vs)])
                    yield
                    cb = cosT[:, b, :].unsqueeze(1).to_broadcast([128, 8, 32])
                    sbq = sinT[:, b, :].unsqueeze(1).to_broadcast([128, 8, 32])
                    cb2 = cosT[:, b, :].unsqueeze(1).to_broadcast([128, 2, 32])
                    sb2 = sinT[:, b, :].unsqueeze(1).to_broadcast([128, 2, 32])
                    q3 = qp[:, :].rearrange("p (h d) -> p h d", h=8)
                    k3 = kvp[:, 0:128].rearrange("p (h d) -> p h d", h=2)
                    TT = lambda out, a, bb, op, reads, writes: P.add(
                        "vector", lambda e: e.tensor_tensor(out=out, in0=a, in1=bb, op=op), reads=reads + [AR], writes=writes)
                    ropeC = ysq[0].rearrange("p (h d) -> p h d", h=8)
                    ropeD = ysq[1].rearrange("p (h d) -> p h d", h=8)
                    YC, YD = ("ysq", 0), ("ysq", 1)
                    TT(ropeA, q3[:, :, 0:32], cb, ALU.mult, [tq, "cos"], ["ropeA"])
                    TT(ropeB, q3[:, :, 32:64], sbq, ALU.mult, [tq, "sin"], ["ropeB"])
                    TT(ropeC, q3[:, :, 32:64], cb, ALU.mult, [tq, "cos"], [YC])
                    TT(ropeD, q3[:, :, 0:32], sbq, ALU.mult, [tq, "sin"], [YD])
                    TT(q_r[:, :, 0:32], ropeA, ropeB, ALU.subtract, ["ropeA", "ropeB"], [("q_r", 0)])
                    TT(q_r[:, :, 32:64], ropeC, ropeD, ALU.add, [YC, YD], [("q_r", 1)])
                    rA2, rB2, rC2, rD2 = ropeA[:, 0:2, :], ropeB[:, 0:2, :], ropeC[:, 0:2, :], ropeD[:, 0:2, :]
                    TT(rA2, k3[:, :, 0:32], cb2, ALU.mult, [tkv, "cos"], ["ropeA"])
                    TT(rB2, k3[:, :, 32:64], sb2, ALU.mult, [tkv, "sin"], ["ropeB"])
                    TT(rC2, k3[:, :, 32:64], cb2, ALU.mult, [tkv, "cos"], [YC])
                    TT(rD2, k3[:, :, 0:32], sb2, ALU.mult, [tkv, "sin"], [YD])
                    TT(k_r[:, :, 0:32], rA2, rB2, ALU.subtract, ["ropeA", "ropeB"], [("k_r", 0)])
                    TT(k_r[:, :, 32:64], rC2, rD2, ALU.add, [YC, YD], [("k_r", 1)])
                    yield
                    pq, tpq = pb()
                    for h in range(8):
                        P.add("tensor", lambda e, h=h: e.transpose(pq[0:64, h, :], q_r[:, h, :], ident[:]),
                              reads=[("q_r", 0), ("q_r", 1), "ident", AR], writes=[tpq])
                    P.add("scalar", lambda e: e.copy(out=qT[0:64, :], in_=pq[0:64, :, :].rearrange("p h t -> p (h t)")),
                          reads=[tpq, AR], writes=["qT"])
                    pk, tpk = pb()
                    for g in range(2):
                        P.add("tensor", lambda e, g=g: e.transpose(pk[0:64, g, :], k_r[:, g, :], ident[:]),
                              reads=[("k_r", 0), ("k_r", 1), "ident", AR], writes=[tpk])
                    P.add("scalar", lambda e: e.copy(out=kT[par][0:64, :, :], in_=pk[0:64, 0:2, :]),
                          reads=[tpk, AR], writes=[("kT", par)])
                    if light:
                        return
                    yield
                    kbs = ([] if b == 0 else [(0, 1 - par, 2 if b == 2 else 1, vsp)]) + [(1, par, 0, vs)]
                    for g in range(2):
                        for (ki, kpar, mi, vsl) in kbs:
                            stb, tst = pf()
                            P.add("tensor", lambda e, g=g, kpar=kpar, stb=stb: e.matmul(
                                stb[:, :], lhsT=kT[kpar][:, g, :], rhs=qT[:, g * 512:(g + 1) * 512],
                                start=True, stop=False), reads=[("kT", kpar), "qT", AR], writes=[tst])
                            P.add("tensor", lambda e, mi=mi, stb=stb: e.matmul(
                                stb[:, :], lhsT=ident[:], rhs=maskS[:, mi, :], start=False, stop=True),
                                reads=["ident", "mask"], writes=[tst])
                            P.add("scalar", lambda e, g=g, ki=ki, stb=stb: e.activation(
                                out=PT[:, g, ki, :], in_=stb[:, :], func=AF.Exp, scale=0.125),
                                reads=[tst, AR], writes=[("PT", g, ki)])
                    yield
                    ovs = []
                    for g in range(2):
                        ov, tov = pf()
                        ovs.append((ov, tov))
                        for hl in range(4):
                            for n, (ki, kpar, mi, vsl) in enumerate(kbs):
                                P.add("tensor", lambda e, g=g, hl=hl, ki=ki, vsl=vsl, ov=ov, n=n, last=(n == len(kbs) - 1): e.matmul(
                                    ov[:, hl * 65:(hl + 1) * 65], lhsT=PT[:, g, ki, hl * 128:(hl + 1) * 128],
                                    rhs=vx[:, vsl, g, :], start=(n == 0), stop=last),
                                    reads=[("PT", g, ki), ("vx", vsl), AR], writes=[tov])
                        ov3 = ov[:, 0:260].rearrange("p (h d) -> p h d", h=4)
                        P.add("vector", lambda e, g=g, ov3=ov3: e.tensor_tensor(
                            out=den[:, 4 * g:4 * g + 4], in0=ov3[:, :, 64], in1=esink[:, 4 * g:4 * g + 4], op=ALU.add),
                            reads=[tov, "esink"], writes=[("den", g)])
                    P.add("vector", lambda e: e.reciprocal(out=den[:], in_=den[:]), reads=[("den", 0), ("den", 1)],
                          writes=[("den", 0), ("den", 1)])
                    for g in range(2):
                        ov, tov = ovs[g]
                        ov3 = ov[:, 0:260].rearrange("p (h d) -> p h d", h=4)
                        P.add("vector", lambda e, g=g, ov3=ov3: e.tensor_tensor(
                            out=attn_tok[:, 4 * g:4 * g + 4, :], in0=ov3[:, :, 0:64],
                            in1=den[:, 4 * g:4 * g + 4].unsqueeze(2).to_broadcast([128, 4, 64]), op=ALU.mult),
                            reads=[tov, ("den", g), AR], writes=[("atok", g)])
                    yield
                    pa, tpa = pb()
                    for c in range(4):
                        P.add("tensor", lambda e, c=c: e.transpose(pa[:, c, :], attn_tok2[:, c * 128:(c + 1) * 128], ident[:]),
                              reads=[("atok", 0), ("atok", 1), "ident", AR], writes=[tpa])
                    P.add("scalar", lambda e: e.copy(out=attnT[:, :, bl * 128:(bl + 1) * 128], in_=pa[:, 0:4, :]),
                          reads=[tpa, AR], writes=[("attnT", bl)])

                bg = [block_gen(bl, b) for bl, b in enumerate(blocks)]

                def step(i):
                    try:
                        next(bg[i])
                    except StopIteration:
                        pass

                def cpiece():
                    if conv_pieces:
                        conv_pieces.pop(0)()
                step(0)
                yield
                for item in (1, 0, 0, "c", 1, 0, 1, "c", 0, 1, "c", 0, 1, "c", 1, 0, 1):
                    if item == "c":
                        cpiece()
                    else:
                        step(item)
                if light:
                    yield
                    return
                while conv_pieces:
                    conv_pieces.pop(0)()
                b1, t1 = pf()
                b2, t2 = pf()
                for c in range(4):
                    sp = c % 2
                    P.add("scalar", lambda e, c=c, sp=sp: e.activation(out=ysq[sp], in_=acc[:, c, :], func=AF.Square),
                          reads=[("acc", c), AR], writes=[("ysq", sp)])
                    P.add("tensor", lambda e, c=c, b1=b1: e.matmul(b1[:, 0:256], lhsT=onesf[:], rhs=acc[:, c, :], start=(c == 0), stop=(c == 3)),
                          reads=[("acc", c), "onesf", AR], writes=[t1])
                    P.add("tensor", lambda e, c=c, sp=sp, b2=b2: e.matmul(b2[:, 0:256], lhsT=onesf[:], rhs=ysq[sp], start=(c == 0), stop=(c == 3)),
                          reads=[("ysq", sp), "onesf", AR], writes=[t2])
                yield
                P.add("vector", lambda e, b1=b1: e.tensor_scalar(mean, b1[:, 0:256], 1.0 / 512, None, ALU.mult), reads=[t1, AR], writes=["mean"])
                P.add("vector", lambda e: e.tensor_tensor(out=var, in0=mean, in1=mean, op=ALU.mult), reads=["mean", AR], writes=["var"])
                P.add("vector", lambda e, b2=b2: e.scalar_tensor_tensor(out=var, in0=b2[:, 0:256], scalar=1.0 / 512, in1=var,
                                                                 op0=ALU.mult, op1=ALU.subtract), reads=[t2, "var", AR], writes=["var"])
                P.add("scalar", lambda e: e.activation(out=var, in_=var, func=AF.Sqrt, bias=junk[:, 1:2]), reads=["var", "epsT", AR], writes=["var"])
                P.add("vector", lambda e: e.reciprocal(out=rstdb, in_=var), reads=["var", AR], writes=["rstdb"])
                for c in range(4):
                    tp, tpt = (tmpc[0], "tmpc0") if c % 2 == 0 else (vf(2944, 3200), "ropeA")
                    P.add("vector", lambda e, c=c, tp=tp: e.tensor_tensor(out=tp, in0=acc[:, c, :], in1=mean, op=ALU.subtract),
                          reads=[("acc", c), "mean", AR], writes=[tpt])
                    P.add("vector", lambda e, tp=tp: e.tensor_tensor(out=tp, in0=tp, in1=rstdb, op=ALU.mult),
                          reads=[tpt, "rstdb", AR], writes=[tpt])
                    P.add("scalar", lambda e, c=c, tp=tp: e.activation(out=convo[:, c, :], in_=tp, func=AF.Silu,
                                                                        scale=cv(c, 32), bias=cv(c, 33)),
                          reads=[tpt, "cvec", AR], writes=[("convo", c)])
                xos = {}
                for bl, b in enumerate(blocks):
                    for half in range(2):
                        xo, txo = pf()
                        xos[(bl, half)] = (xo, txo)
                        for c in range(4):
                            P.add("tensor", lambda e, c=c, bl=bl, half=half, xo=xo: e.matmul(
                                xo[:, :], lhsT=attnT[:, c, bl * 128:(bl + 1) * 128], rhs=wout_v[:, c, half * 512:(half + 1) * 512],
                                start=(c == 0), stop=False), reads=[("attnT", bl), AR] + WO, writes=[txo])
                yield
                for bl, b in enumerate(blocks):
                    for half in range(2):
                        xo, txo = xos[(bl, half)]
                        for c in range(4):
                            P.add("tensor", lambda e, c=c, bl=bl, half=half, xo=xo: e.matmul(
                                xo[:, :], lhsT=convo[:, c, bl * 128:(bl + 1) * 128], rhs=wout_v[:, 4 + c, half * 512:(half + 1) * 512],
                                start=False, stop=(c == 3)), reads=[("convo", c), AR] + WO, writes=[txo])
                        P.add("vector", lambda e, b=b, half=half, xo=xo: e.tensor_tensor(
                            out=xs[:, b, half * 512:(half + 1) * 512], in0=xo[:, :], in1=xs[:, b, half * 512:(half + 1) * 512], op=ALU.add),
                            reads=[txo, ("x", b)], writes=[("x", b)])

            gens = [group_gen(gi, blocks) for gi, blocks in enumerate(mix_groups)]

            def adv(i):
                if i < len(gens):
                    try:
                        next(gens[i])
                    except StopIteration:
                        pass
            adv(0)
            adv(0)
            adv(0)
            for gi in range(len(gens)):
                adv(gi + 1)
                adv(gi)
                adv(gi + 1)
                adv(gi)
                adv(gi + 1)
                for _ in gens[gi]:
                    pass

        gfin_loaded = [False]

        def final_out(blocks):
            if not gfin_loaded[0]:
                gfin_loaded[0] = True
                P.add("sync", lambda e: e.dma_start(out=gfin_v, in_=gfin_d.partition_broadcast(128)), reads=[AR], writes=["gfin"], dma=True)
            for b in blocks:
                if b < 2:
                    continue
                i = b - 2
                P.add("scalar", lambda e, b=b: e.activation(out=hn[0], in_=xs[:, b, :], func=AF.Square, accum_out=ss[:, b:b + 1]),
                      reads=[("x", b), AR], writes=[("hn", 0), ("ss", b)])
                P.add("vector", lambda e, b=b: e.tensor_scalar(rs[:, b:b + 1], ss[:, b:b + 1], 1.0 / D, EPS, ALU.mult, ALU.add),
                      reads=[("ss", b)], writes=[("rs", b)])
                P.add("scalar", lambda e, b=b: e.activation(out=rs[:, b:b + 1], in_=rs[:, b:b + 1], func=AF.Sqrt),
                      reads=[("rs", b)], writes=[("rs", b)])
                P.add("vector", lambda e, b=b: e.reciprocal(out=rs[:, b:b + 1], in_=rs[:, b:b + 1]),
                      reads=[("rs", b)], writes=[("rs", b)])
                P.add("vector", lambda e, b=b: e.scalar_tensor_tensor(out=xs[:, b, :], in0=xs[:, b, :], scalar=rs[:, b:b + 1],
                                                                      in1=gfin_v, op0=ALU.mult, op1=ALU.mult),
                      reads=[("x", b), ("rs", b), "gfin", AR], writes=[("x", b)])
                P.add("sync", lambda e, i=i, b=b: e.dma_start(out=out_d[i * 128:(i + 1) * 128, :], in_=xs[:, b, :]),
                      reads=[("x", b)], dma=True)

        allb = list(range(NB))
        stage = [0]

        def done():
            stage[0] += 1
            return stop_after is not None and stage[0] >= stop_after

        finished = False
        import os as _os
        KS = int(_os.environ.get("KSTOP", "0"))
        for l in range(DEPTH):
            if KS == 1:
                finished = True
                break
            if KS == 2:
                finished = True
                break
            ffn(l, "ffn1_w_gate", "ffn1_w_up", "ffn1_w_down", pre_norm=(0 if l == 0 else None), first_block=(0 if l == 0 else 1))
            if done():
                finished = True
                break
            barrier()
            mixer(l)
            barrier()
            if done():
                finished = True
                break
            last = (l + 1 == DEPTH) and stop_after is None
            ffn(l, "ffn2_w_gate", "ffn2_w_up", "ffn2_w_down", post_norm=(3 * (l + 1) if l + 1 < DEPTH else None),
                pre_norm=3 * l + 2, post_hook=(final_out if last else None), first_block=(2 if l + 1 == DEPTH else 1))
            if done():
                finished = True
                break
        barrier()
        if finished:
            for i in range(NOWN):
                b = i + 2
                P.add("sync", lambda e, i=i, b=b: e.dma_start(out=out_d[i * 128:(i + 1) * 128, :], in_=xs[:, b, :]),
                      reads=[("x", b)], dma=True)
        else:
            pass
        P.emit(nc)
    return nc


def make_inputs(inputs, n_cores, cores_per_seq, NOWN, DEPTH):
    x = np.asarray(inputs["x"], np.float32)
    pos = np.asarray(inputs["positions"], np.int32)
    B, S, _ = x.shape
    own = NOWN * 128
    NB = NOWN + 2
    bf = ml_dtypes.bfloat16
    i = np.arange(128)[None, :]
    j = np.arange(128)[:, None]
    m_cur = np.where(i >= j, 0.0, NEG).astype(np.float32)
    m_prev = np.where(j > i, 0.0, NEG).astype(np.float32)
    m_none = np.full((128, 128), NEG, np.float32)
    inv_freq = (1.0 / (10000.0 ** (np.arange(0, 64, 2, dtype=np.float32) / 64))).astype(np.float32)
    invf = np.broadcast_to((inv_freq.astype(np.float64) / (2 * np.pi)).astype(np.float32)[None, :], (128, 32)).copy()
    gains = np.zeros((128, 3 * DEPTH, 8), np.float32)
    for l in range(DEPTH):
        for k, n in enumerate(["ffn1_norm", "mix_norm", "ffn2_norm"]):
            gains[:, 3 * l + k, :] = np.asarray(inputs[n], np.float32)[l].reshape(8, 128).T
    cvec = np.zeros((128, DEPTH, 4, 34), np.float32)
    for l in range(DEPTH):
        cw = np.asarray(inputs["conv_w"], np.float32)[l]
        cvec[:, l, :, 0:31] = cw.T.reshape(4, 128, 31).transpose(1, 0, 2)
        for k, n in ((31, "conv_b"), (32, "conv_ln_g"), (33, "conv_ln_b")):
            cvec[:, l, :, k] = np.asarray(inputs[n], np.float32)[l].reshape(4, 128).T
    shared = {
        "ident": np.eye(128).astype(bf), "invf": invf, "gains": gains, "cvec": cvec,
        "gfin": np.asarray(inputs["final_norm"], np.float32),
        "sinks": np.asarray(inputs["attn_sinks"], np.float32),
    }
    for n in WNAMES:
        shared[n] = np.ascontiguousarray(np.asarray(inputs[n], np.float32))
    in_maps = []
    for core in range(n_cores):
        bi, ci = divmod(core, cores_per_seq)
        t0 = ci * own
        xc = np.zeros((NB * 128, D), np.float32)
        pc = np.zeros((NB * 128,), np.int32)
        lo = t0 - 256
        if lo >= 0:
            xc[:] = x[bi, lo:t0 + own]
            pc[:] = pos[bi, lo:t0 + own]
            start = False
        else:
            xc[256:] = x[bi, t0:t0 + own]
            pc[256:] = pos[bi, t0:t0 + own]
            start = True
        mask = np.stack([m_cur, m_prev, m_none if start else m_prev], axis=1)
        mask = np.repeat(mask[:, :, None, :], 4, axis=2).reshape(128, 3, 512).astype(bf)
        hvv = np.ones((128, 2), np.float32)
        hvv[:, 0] = 0.0 if start else 1.0
        m = dict(shared)
        m.update({"x": xc, "pos": np.ascontiguousarray(pc.reshape(NB, 128).T), "mask": mask, "hv": hvv})
        in_maps.append(m)
    return in_maps


def run(inputs, n_cores, cores_per_seq, NOWN, DEPTH=2, stop_after=None, trace=False):
    nc = build(NOWN, DEPTH, stop_after)
    in_maps = make_inputs(inputs, n_cores, cores_per_seq, NOWN, DEPTH)
    res = run_bass_kernel_spmd(nc, in_maps, core_ids=list(range(n_cores)), trace=trace)
    x = np.asarray(inputs["x"])
    B, S, _ = x.shape
    own = NOWN * 128
    out = np.zeros((B, S, D), np.float32)
    for core in range(n_cores):
        bi, ci = divmod(core, cores_per_seq)
        out[bi, ci * own:(ci + 1) * own] = res.results[core]["out"]
    return out, res


def kernel(**inputs):
    out, _ = run(inputs, 8, 4, 16, 2)
    return out
```
